# Optimizing a Trainium2 kernel written in Bass

```python
import math
import jax, jax.numpy as jnp
from jax import lax
import numpy as np

D_MODEL = 2048
BATCH = 2
SEQ = 8192
DEPTH = 1

D_MIX = D_MODEL
D_LRU = D_MIX // 2
D_SC = D_MIX - D_LRU
N_LRU_HEADS = 8
LRU_HEAD_DIM = D_LRU // N_LRU_HEADS
N_SC_HEADS = 8
LRU_CONV_WIDTH = 4
SC_CONV_WIDTH = 3
LRU_C = 8.0
D_FF = 5632
FFN_RESIDUAL_SCALE = 0.5
NORM_EPS = 1e-6
D_IN_PROJ = 2 * D_LRU + 3 * D_SC

kernel_name = "hawk_shortconv_macaron_hybrid"


def rms_norm(x, gain):
    xf = x.astype(jnp.float32)
    var = jnp.mean(xf * xf, axis=-1, keepdims=True)
    return (xf * lax.rsqrt(var + NORM_EPS) * gain.astype(jnp.float32)).astype(x.dtype)


def swiglu_ffn(x, w_gate, w_up, w_down):
    return (jax.nn.silu(x @ w_gate) * (x @ w_up)) @ w_down


def causal_depthwise_conv(x, w):
    K = w.shape[0]
    S = x.shape[1]
    xp = jnp.pad(x, ((0, 0), (K - 1, 0), (0, 0)))
    y = xp[:, 0:S] * w[0]
    for k in range(1, K):
        y = y + xp[:, k:k + S] * w[k]
    return y


def _lru_combine(left, right):
    a_l, b_l = left
    a_r, b_r = right
    return a_l * a_r, a_r * b_l + b_r


def rg_lru(x, w_a, b_a, w_i, b_i, lam):
    Bsz, S, W = x.shape
    xh = x.reshape(Bsz, S, N_LRU_HEADS, LRU_HEAD_DIM)
    r = jax.nn.sigmoid(jnp.einsum('bshi,hij->bshj', xh, w_a) + b_a).reshape(Bsz, S, W)
    i = jax.nn.sigmoid(jnp.einsum('bshi,hij->bshj', xh, w_i) + b_i).reshape(Bsz, S, W)
    log_a = -LRU_C * r.astype(jnp.float32) * jax.nn.softplus(-lam.astype(jnp.float32))
    a = jnp.exp(log_a)
    mult = jnp.sqrt(-jnp.expm1(2.0 * log_a))
    u = mult * (i * x).astype(jnp.float32)
    _, h = lax.associative_scan(_lru_combine, (a, u), axis=1)
    return h.astype(x.dtype)


def setup_inputs(seed: int = 0) -> dict:
    key = jax.random.key(seed)
    ks = jax.random.split(key, 32)
    f32 = jnp.float32

    def normal(k, shape, fan_in):
        return jax.random.normal(k, shape, f32) * (fan_in ** -0.5)

    def gain(k, shape):
        return 1.0 + 0.02 * jax.random.normal(k, shape, f32)

    def small(k, shape):
        return 0.01 * jax.random.normal(k, shape, f32)

    L = DEPTH
    a0 = jax.random.uniform(ks[13], (L, D_LRU), f32, 0.9, 0.999) ** (1.0 / LRU_C)
    lru_lambda = jnp.log(a0) - jnp.log1p(-a0)
    return {
        "x": jax.random.normal(ks[0], (BATCH, SEQ, D_MODEL), f32),
        "ffn1_norm": gain(ks[1], (L, D_MODEL)),
        "ffn1_w_gate": normal(ks[2], (L, D_MODEL, D_FF), D_MODEL),
        "ffn1_w_up": normal(ks[3], (L, D_MODEL, D_FF), D_MODEL),
        "ffn1_w_down": normal(ks[4], (L, D_FF, D_MODEL), D_FF),
        "mix_norm": gain(ks[5], (L, D_MODEL)),
        "w_in": normal(ks[6], (L, D_MODEL, D_IN_PROJ), D_MODEL),
        "lru_conv_w": normal(ks[7], (L, LRU_CONV_WIDTH, D_LRU), LRU_CONV_WIDTH),
        "lru_conv_b": small(ks[8], (L, D_LRU)),
        "lru_w_a": normal(ks[9], (L, N_LRU_HEADS, LRU_HEAD_DIM, LRU_HEAD_DIM), LRU_HEAD_DIM),
        "lru_b_a": small(ks[10], (L, N_LRU_HEADS, LRU_HEAD_DIM)),
        "lru_w_i": normal(ks[11], (L, N_LRU_HEADS, LRU_HEAD_DIM, LRU_HEAD_DIM), LRU_HEAD_DIM),
        "lru_b_i": small(ks[12], (L, N_LRU_HEADS, LRU_HEAD_DIM)),
        "lru_lambda": lru_lambda,
        "sc_conv_w": normal(ks[14], (L, SC_CONV_WIDTH, D_SC), SC_CONV_WIDTH),
        "lru_out_norm": gain(ks[15], (L, D_LRU)),
        "sc_out_norm": gain(ks[16], (L, D_SC)),
        "w_out": normal(ks[17], (L, D_MIX, D_MODEL), D_MIX),
        "ffn2_norm": gain(ks[18], (L, D_MODEL)),
        "ffn2_w_gate": normal(ks[19], (L, D_MODEL, D_FF), D_MODEL),
        "ffn2_w_up": normal(ks[20], (L, D_MODEL, D_FF), D_MODEL),
        "ffn2_w_down": normal(ks[21], (L, D_FF, D_MODEL), D_FF),
        "final_norm": gain(ks[22], (D_MODEL,)),
    }


def reference(x, ffn1_norm, ffn1_w_gate, ffn1_w_up, ffn1_w_down, mix_norm, w_in,
              lru_conv_w, lru_conv_b, lru_w_a, lru_b_a, lru_w_i, lru_b_i, lru_lambda,
              sc_conv_w, lru_out_norm, sc_out_norm, w_out,
              ffn2_norm, ffn2_w_gate, ffn2_w_up, ffn2_w_down, final_norm):
    for l in range(DEPTH):
        x = x + FFN_RESIDUAL_SCALE * swiglu_ffn(rms_norm(x, ffn1_norm[l]),
                                                ffn1_w_gate[l], ffn1_w_up[l], ffn1_w_down[l])
        z = rms_norm(x, mix_norm[l]) @ w_in[l]
        o = 0
        lru_x = z[..., o:o + D_LRU]; o += D_LRU
        lru_gate = z[..., o:o + D_LRU]; o += D_LRU
        sc_b = z[..., o:o + D_SC]; o += D_SC
        sc_c = z[..., o:o + D_SC]; o += D_SC
        sc_x = z[..., o:o + D_SC]
        xc = causal_depthwise_conv(lru_x, lru_conv_w[l]) + lru_conv_b[l]
        h = rg_lru(xc, lru_w_a[l], lru_b_a[l], lru_w_i[l], lru_b_i[l], lru_lambda[l])
        y_lru = h * jax.nn.gelu(lru_gate, approximate=True)
        y_sc = sc_b * causal_depthwise_conv(sc_c * sc_x, sc_conv_w[l])
        y = jnp.concatenate([rms_norm(y_lru, lru_out_norm[l]),
                             rms_norm(y_sc, sc_out_norm[l])], axis=-1)
        x = x + y @ w_out[l]
        x = x + FFN_RESIDUAL_SCALE * swiglu_ffn(rms_norm(x, ffn2_norm[l]),
                                                ffn2_w_gate[l], ffn2_w_up[l], ffn2_w_down[l])
    return rms_norm(x, final_norm)
```

```python
import numpy as np
from contextlib import ExitStack
import concourse.bass as bass
import concourse.mybir as mybir
from concourse.bass_utils import run_bass_kernel_spmd

F32 = mybir.dt.float32
BF16 = mybir.dt.bfloat16
AF = mybir.ActivationFunctionType
ALU = mybir.AluOpType

D = 2048
DFF = 5632
NK = 16
NF = 44
FGS = 4
NG = 11
TS = 512
NS = 2
TH = 1024
NH = 8
NPRE = 6
NW = 4
EPS = 1e-6

V_G1, V_GMIX, V_G2, V_GFIN = 0, 16, 32, 48
V_GLRU, V_GSC = 64, 72
V_CW = 80
V_CB = 112
V_BA = 120
V_BI = 128
V_LAM = 136
V_SCW = 144
V_MASK = 168
NV = 184
DV_HBA, DV_HBI, DV_N4, DV_N8 = 0, 8, 16, 24
NDV = 32


_DBG = {}


class Buf:
    __slots__ = ("w", "r")

    def __init__(self):
        self.w = None
        self.r = {}


class Sched:
    def __init__(self):
        self.ops = {e: [] for e in ("pe", "act", "dve", "pool", "sp")}
        self.cnt = {"pe": 0, "act": 0, "dve": 0}
        self.waited = {e: {} for e in self.ops}
        self.dmacnt = {}

    def _deps(self, eng, reads, writes, strict=False):
        d = {}

        def add(tok):
            if tok is None:
                return
            k, v = tok
            if k == ("p", eng):
                if not (strict and self.cnt[eng] - v < 3):
                    return
            if d.get(k, 0) < v:
                d[k] = v

        for b in reads:
            add(b.w)
        for b in writes:
            add(b.w)
            for k, v in b.r.items():
                add((k, v))
        waits = []
        for k, v in d.items():
            if self.waited[eng].get(k, 0) >= v:
                continue
            self.waited[eng][k] = v
            waits.append((k, v))
        return waits

    def _commit(self, tok, reads, writes):
        k, v = tok
        for b in reads:
            if b.r.get(k, 0) < v:
                b.r[k] = v
        for b in writes:
            b.w = tok
            b.r = {}

    def op(self, eng, fn, reads=(), writes=(), strict=False):
        waits = self._deps(eng, reads, writes, strict)
        self.cnt[eng] += 1
        tok = (("p", eng), self.cnt[eng])
        self.ops[eng].append((waits, fn, (("p", eng), 1)))
        self._commit(tok, reads, writes)
        return tok

    def dma(self, eng, semkey, fn, reads=(), writes=()):
        waits = self._deps(eng, reads, writes)
        self.dmacnt[semkey] = self.dmacnt.get(semkey, 0) + 16
        tok = (semkey, self.dmacnt[semkey])
        self.ops[eng].append((waits, fn, (semkey, 16)))
        self._commit(tok, reads, writes)
        return tok

    def mm_group(self, items, writes):
        n = len(items)
        allr = []
        for i, (fn, reads) in enumerate(items):
            waits = self._deps("pe", reads, writes if i == 0 else ())
            last = i == n - 1
            if last:
                self.cnt["pe"] += 1
            self.ops["pe"].append((waits, fn, (("p", "pe"), 1) if last else None))
            allr.extend(reads)
        tok = (("p", "pe"), self.cnt["pe"])
        self._commit(tok, allr, writes)
        return tok


def build_program(debug=False):
    nc = bass.Bass("TRN2", target_bir_lowering=False)
    S = Sched()

    xs = nc.dram_tensor("xs", [NH, 128, NK * TH], F32, kind="ExternalInput").ap()
    wgu = [nc.dram_tensor("wgu%d" % i, [NF, 128, 4096], F32, kind="ExternalInput").ap() for i in (1, 2)]
    wdn = [nc.dram_tensor("wd%d" % i, [2 * NG, 128, 4096], F32, kind="ExternalInput").ap() for i in (1, 2)]
    win = nc.dram_tensor("win", [20, 128, 4096], F32, kind="ExternalInput").ap()
    wout = nc.dram_tensor("wout", [8, 128, 4096], F32, kind="ExternalInput").ap()
    wgate = nc.dram_tensor("wgate", [128, 2048], F32, kind="ExternalInput").ap()
    vecd = nc.dram_tensor("vec", [128, NV], F32, kind="ExternalInput").ap()
    outd = nc.dram_tensor("out", [2, 128, NK * TH], F32, kind="ExternalOutput").ap()

    dbg_list = []

    def dbg(name, ap, shape, dt, bufs):
        if not debug or wstate["dry"]:
            return
        d = nc.dram_tensor(name, shape, dt, kind="ExternalOutput").ap()
        dbg_list.append(name)
        S.dma("sp", ("dbg", len(dbg_list)), lambda e: e.dma_start(out=d, in_=ap), reads=tuple(bufs), writes=())

    es = ExitStack()
    with es:
        def sb(name, shape, dt):
            return es.enter_context(nc.sbuf_tensor(name, shape, dt))

        X = sb("X", [128, NK, TH], F32)
        XN = sb("XN", [128, NK, TH], BF16)
        H = sb("H", [128, 2, FGS, TH], BF16)
        W = sb("W", [128, NW, 4096], BF16)
        Y = sb("Y", [128, NK, TS], F32)
        NT = 9
        T = [sb("T%d" % i, [128, TS + 4], F32) for i in range(NT)]
        XCB = sb("XCB", [128, TS], BF16)
        SQ = [sb("SQ%d" % i, [128, TS], BF16) for i in range(2)]
        WG = sb("WG", [128, 2, 8, 128], BF16)
        VEC = sb("VEC", [128, NV], F32)
        DV = sb("DV", [128, NDV], F32)
        ONES = sb("ONES", [128, 128], BF16)
        LC = sb("LC", [128, 8, 4], F32)
        SCC = sb("SCC", [128, 8, 2], F32)
        HST = sb("HST", [128, 8], F32)
        CE = sb("CE", [128, 1], F32)
        CQ = sb("CQ", [128, 1], F32)
        SPT = sb("SPT", [128, 8], F32)
        PS = [es.enter_context(nc.psum_tensor("ps%d" % i, [128, TS], F32)) for i in range(8)]

        sems = {}

        def sem(key):
            if key not in sems:
                sems[key] = es.enter_context(nc.semaphore("s%d" % len(sems)))
            return sems[key]

        for k in (("p", "pe"), ("p", "act"), ("p", "dve")):
            sem(k)

        bX = [[Buf() for _ in range(NS)] for _ in range(NK)]
        bXN = [[Buf() for _ in range(NS)] for _ in range(NK)]
        bH = [[[Buf() for _ in range(NS)] for _ in range(FGS)] for _ in range(2)]
        bW = [Buf() for _ in range(NW)]
        bY = [Buf() for _ in range(NK)]
        bT = [Buf() for _ in range(NT)]
        bXCB = Buf()
        bSQ = [Buf(), Buf()]
        bWG = Buf()
        bVEC = Buf()
        bDV = Buf()
        bONES = Buf()
        bLC = Buf()
        bSCC = Buf()
        bHST = Buf()
        bPS = [Buf() for _ in range(8)]

        def col(c0, n=1):
            return VEC[:, c0:c0 + n]

        def dcol(c0, n=1):
            return DV[:, c0:c0 + n]

        ps_state = {"i": 0}

        def next_ps():
            i = ps_state["i"]
            ps_state["i"] = (i + 1) % 8
            return i

        wseq = []
        wstate = {"next_use": 0, "next_dma": 0, "dry": True}

        def issue_wdma():
            i = wstate["next_dma"]
            if i >= len(wseq):
                return
            wstate["next_dma"] = i + 1
            slot = i % NW
            src = wseq[i]
            S.dma("pool", ("w", slot),
                  lambda e, slot=slot, src=src: e.dma_start(out=W[:, slot, :], in_=src, max_dma_last_dim=8192),
                  reads=(), writes=(bW[slot],))

        def get_w(src):
            i = wstate["next_use"]
            wstate["next_use"] = i + 1
            if wstate["dry"]:
                wseq.append(src)
            else:
                assert wstate["next_dma"] > i, (i, wstate["next_dma"])
            return i % NW

        def done_w():
            if not wstate["dry"]:
                issue_wdma()

        def emit(eng, fn, reads=(), writes=(), strict=False):
            if wstate["dry"]:
                return
            S.op(eng, fn, reads, writes, strict)

        def emit_group(items, writes):
            if wstate["dry"]:
                return
            S.mm_group(items, writes)

        def mm(out_ap, lhsT, rhs, start, stop):
            return lambda e: e.matmul(out_ap, lhsT, rhs, start=start, stop=stop)

        def sl(s):
            return slice(s * TS, (s + 1) * TS)

        def norm_generic(src_ap_fn, src_bufs, nchunks, inv_n, apply_fn):
            p = next_ps()
            if not wstate["dry"]:
                for k in range(nchunks):
                    q = k % 2
                    S.op("act", lambda e, k=k, q=q: e.activation(out=SQ[q][:, :], in_=src_ap_fn(k), func=AF.Square),
                         reads=(src_bufs[k],), writes=(bSQ[q],))
                    waits = S._deps("pe", (bSQ[q], bONES), (bPS[p],) if k == 0 else ())
                    last = k == nchunks - 1
                    S.cnt["pe"] += 1
                    tok = (("p", "pe"), S.cnt["pe"])
                    S.ops["pe"].append((waits, mm(PS[p][:, :], ONES[:, :], SQ[q][:, :], k == 0, last), (("p", "pe"), 1)))
                    S._commit(tok, (bSQ[q], bONES), (bPS[p],) if last else ())
                    if not last:
                        pass
                S.op("act", lambda e: e.activation(out=T[8][:, 0:TS], in_=PS[p][:, :], func=AF.Sqrt, scale=inv_n, bias=EPSC[:, 0:1]),
                     reads=(bPS[p], bDV), writes=(bT[8],))
                S.op("dve", lambda e: e.reciprocal(out=T[8][:, 0:TS], in_=T[8][:, 0:TS]), reads=(bT[8],), writes=(bT[8],))
            apply_fn(T[8][:, 0:TS], bT[8])

        EPSC = CE[:, 0:1]

        def norm_x_to_xn(gbase):
            for s in range(NS):
                def apply(rs, brs, s=s):
                    for k in range(NK):
                        emit("dve", lambda e, k=k, s=s: e.scalar_tensor_tensor(
                            out=XN[:, k, sl(s)], in0=X[:, k, sl(s)], scalar=col(gbase + k), in1=rs,
                            op0=ALU.mult, op1=ALU.mult),
                            reads=(bX[k][s], brs, bVEC), writes=(bXN[k][s],))
                norm_generic(lambda k, s=s: X[:, k, sl(s)], [bX[k][s] for k in range(NK)], NK, 1.0 / D, apply)

        def norm_x_final(gbase):
            for s in range(NS):
                def apply(rs, brs, s=s):
                    for k in range(NK):
                        emit("dve", lambda e, k=k, s=s: e.scalar_tensor_tensor(
                            out=X[:, k, sl(s)], in0=X[:, k, sl(s)], scalar=col(gbase + k), in1=rs,
                            op0=ALU.mult, op1=ALU.mult),
                            reads=(bX[k][s], brs, bVEC), writes=(bX[k][s],))
                norm_generic(lambda k, s=s: X[:, k, sl(s)], [bX[k][s] for k in range(NK)], NK, 1.0 / D, apply)

        def ffn(fi):
            def stage1(g):
                hb = g % 2
                for fl in range(FGS):
                    f = g * FGS + fl
                    slot = get_w(wgu[fi][f])
                    for s in range(NS):
                        pg = next_ps()
                        pu = next_ps()
                        for (t, p) in ((0, pg), (1, pu)):
                            items = []
                            for k in range(NK):
                                o = (t * NK + k) * 128
                                items.append((mm(PS[p][:, :], W[:, slot, o:o + 128], XN[:, k, sl(s)], k == 0, k == NK - 1),
                                              (bW[slot], bXN[k][s])))
                            emit_group(items, (bPS[p],))
                        tq = (f * NS + s) % 2
                        emit("act", lambda e, pg=pg, tq=tq: e.activation(out=T[6 + tq][:, 0:TS], in_=PS[pg][:, :], func=AF.Silu),
                             reads=(bPS[pg],), writes=(bT[6 + tq],))
                        emit("dve", lambda e, pu=pu, tq=tq, hb=hb, fl=fl, s=s: e.tensor_tensor(
                            out=H[:, hb, fl, sl(s)], in0=PS[pu][:, :], in1=T[6 + tq][:, 0:TS], op=ALU.mult),
                            reads=(bPS[pu], bT[6 + tq]), writes=(bH[hb][fl][s],))
                    done_w()

            def stage2(g):
                hb = g % 2
                for mh in range(2):
                    slot = get_w(wdn[fi][g * 2 + mh])
                    for m8 in range(8):
                        m = mh * 8 + m8
                        for s in range(NS):
                            p = next_ps()
                            items = []
                            for fl in range(FGS):
                                o = (fl * 8 + m8) * 128
                                items.append((mm(PS[p][:, :], W[:, slot, o:o + 128], H[:, hb, fl, sl(s)], fl == 0, fl == FGS - 1),
                                              (bW[slot], bH[hb][fl][s])))
                            emit_group(items, (bPS[p],))
                            emit("dve", lambda e, p=p, m=m, s=s: e.scalar_tensor_tensor(
                                out=X[:, m, sl(s)], in0=PS[p][:, :], scalar=0.5, in1=X[:, m, sl(s)],
                                op0=ALU.mult, op1=ALU.add),
                                reads=(bPS[p], bX[m][s]), writes=(bX[m][s],))
                    done_w()

            for g in range(NG):
                stage1(g)
                if g >= 1:
                    stage2(g - 1)
            stage2(NG - 1)

        def lru_chain(c, s, st, pl, pgt, full):
            CB, XC, TR, TI, A2, HS = T[0], T[1], T[2], T[3], T[4], T[5]
            emit("dve", lambda e: e.tensor_copy(out=CB[:, 0:3], in_=LC[:, c, 0:3]), reads=(bLC,), writes=(bT[0],))
            emit("act", lambda e: e.activation(out=CB[:, 3:3 + TS], in_=PS[pl][:, :], func=AF.Copy),
                 reads=(bPS[pl],), writes=(bT[0],))
            emit("dve", lambda e: e.tensor_scalar(out=XC[:, 0:TS], in0=CB[:, 3:3 + TS], scalar1=col(V_CW + 3 * 8 + c),
                                                  scalar2=col(V_CB + c), op0=ALU.mult, op1=ALU.add),
                 reads=(bT[0], bVEC), writes=(bT[1],))
            for tap in (2, 1, 0):
                emit("dve", lambda e, tap=tap: e.scalar_tensor_tensor(
                    out=XC[:, 0:TS], in0=CB[:, tap:tap + TS], scalar=col(V_CW + tap * 8 + c), in1=XC[:, 0:TS],
                    op0=ALU.mult, op1=ALU.add), reads=(bT[0], bT[1], bVEC), writes=(bT[1],))
            emit("dve", lambda e: e.tensor_copy(out=LC[:, c, 0:3], in_=CB[:, TS:TS + 3]), reads=(bT[0],), writes=(bLC,))
            emit("act", lambda e: e.activation(out=XCB[:, :], in_=XC[:, 0:TS], func=AF.Copy), reads=(bT[1],), writes=(bXCB,))
            pr = next_ps()
            pi = next_ps()
            emit_group([(mm(PS[pr][:, :], WG[:, 0, c, :], XCB[:, :], True, True), (bWG, bXCB))], (bPS[pr],))
            emit_group([(mm(PS[pi][:, :], WG[:, 1, c, :], XCB[:, :], True, True), (bWG, bXCB))], (bPS[pi],))
            emit("act", lambda e: e.activation(out=TR[:, 0:TS], in_=PS[pr][:, :], func=AF.Tanh, scale=0.5, bias=dcol(DV_HBA + c)),
                 reads=(bPS[pr], bDV), writes=(bT[2],))
            emit("act", lambda e: e.activation(out=TI[:, 0:TS], in_=PS[pi][:, :], func=AF.Tanh, scale=0.5, bias=dcol(DV_HBI + c)),
                 reads=(bPS[pi], bDV), writes=(bT[3],))
            emit("act", lambda e: e.activation(out=A2[:, 0:TS], in_=TR[:, 0:TS], func=AF.Exp, scale=dcol(DV_N8 + c), bias=dcol(DV_N8 + c)),
                 reads=(bT[2], bDV), writes=(bT[4],))
            emit("act", lambda e: e.activation(out=TR[:, 0:TS], in_=TR[:, 0:TS], func=AF.Exp, scale=dcol(DV_N4 + c), bias=dcol(DV_N4 + c)),
                 reads=(bT[2], bDV), writes=(bT[2],))
            emit("dve", lambda e: e.tensor_scalar(out=A2[:, 0:TS], in0=A2[:, 0:TS], scalar1=0.9999999, scalar2=-0.25,
                                                  op0=ALU.min, op1=ALU.mult), reads=(bT[4],), writes=(bT[4],))
            emit("act", lambda e: e.activation(out=A2[:, 0:TS], in_=A2[:, 0:TS], func=AF.Sqrt, scale=1.0, bias=CQ[:, 0:1]),
                 reads=(bT[4], bDV), writes=(bT[4],))
            emit("dve", lambda e: e.scalar_tensor_tensor(out=TI[:, 0:TS], in0=TI[:, 0:TS], scalar=1.0, in1=XC[:, 0:TS],
                                                         op0=ALU.add, op1=ALU.mult), reads=(bT[3], bT[1]), writes=(bT[3],))
            emit("dve", lambda e: e.tensor_tensor(out=TI[:, 0:TS], in0=TI[:, 0:TS], in1=A2[:, 0:TS], op=ALU.mult),
                 reads=(bT[3], bT[4]), writes=(bT[3],))
            emit("dve", lambda e: e.tensor_tensor_scan(out=HS[:, 0:TS], data0=TR[:, 0:TS], data1=TI[:, 0:TS],
                                                       initial=HST[:, c:c + 1], op0=ALU.mult, op1=ALU.add),
                 reads=(bT[2], bT[3], bHST), writes=(bT[5],))
            emit("dve", lambda e: e.tensor_scalar(out=HST[:, c:c + 1], in0=HS[:, TS - 1:TS], scalar1=col(V_MASK + st),
                                                  scalar2=None, op0=ALU.mult), reads=(bT[5], bVEC), writes=(bHST,), strict=True)
            if full:
                G = A2
                emit("act", lambda e: e.activation(out=G[:, 0:TS], in_=PS[pgt][:, :], func=AF.Square),
                     reads=(bPS[pgt],), writes=(bT[4],))
                emit("dve", lambda e: e.tensor_scalar(out=G[:, 0:TS], in0=G[:, 0:TS], scalar1=0.044715, scalar2=1.0,
                                                      op0=ALU.mult, op1=ALU.add), reads=(bT[4],), writes=(bT[4],))
                emit("dve", lambda e: e.tensor_tensor(out=G[:, 0:TS], in0=PS[pgt][:, :], in1=G[:, 0:TS], op=ALU.mult),
                     reads=(bPS[pgt], bT[4]), writes=(bT[4],))
                emit("act", lambda e: e.activation(out=G[:, 0:TS], in_=G[:, 0:TS], func=AF.Tanh, scale=0.7978845608028654),
                     reads=(bT[4],), writes=(bT[4],))
                emit("dve", lambda e: e.scalar_tensor_tensor(out=G[:, 0:TS], in0=G[:, 0:TS], scalar=1.0, in1=PS[pgt][:, :],
                                                             op0=ALU.add, op1=ALU.mult), reads=(bT[4], bPS[pgt]), writes=(bT[4],))
                emit("dve", lambda e: e.scalar_tensor_tensor(out=Y[:, c, :], in0=G[:, 0:TS], scalar=0.5, in1=HS[:, 0:TS],
                                                             op0=ALU.mult, op1=ALU.mult), reads=(bT[4], bT[5]), writes=(bY[c],))

        def inproj_group(slot, t, s):
            p = next_ps()
            items = []
            for k in range(NK):
                o = (t * NK + k) * 128
                items.append((mm(PS[p][:, :], W[:, slot, o:o + 128], XN[:, k, sl(s)], k == 0, k == NK - 1),
                              (bW[slot], bXN[k][s])))
            emit_group(items, (bPS[p],))
            return p

        def sc_chain(c, s, pc, px, pb, carry_only):
            SX, CB2, Vv = T[0], T[1], T[2]
            emit("act", lambda e: e.activation(out=SX[:, 0:TS], in_=PS[px][:, :], func=AF.Copy), reads=(bPS[px],), writes=(bT[0],))
            emit("dve", lambda e: e.tensor_copy(out=CB2[:, 0:2], in_=SCC[:, c, 0:2]), reads=(bSCC,), writes=(bT[1],))
            emit("dve", lambda e: e.tensor_tensor(out=CB2[:, 2:2 + TS], in0=PS[pc][:, :], in1=SX[:, 0:TS], op=ALU.mult),
                 reads=(bPS[pc], bT[0]), writes=(bT[1],))
            emit("dve", lambda e: e.tensor_copy(out=SCC[:, c, 0:2], in_=CB2[:, TS:TS + 2]), reads=(bT[1],), writes=(bSCC,), strict=True)
            if carry_only:
                return
            emit("dve", lambda e: e.tensor_scalar(out=Vv[:, 0:TS], in0=CB2[:, 2:2 + TS], scalar1=col(V_SCW + 2 * 8 + c),
                                                  scalar2=None, op0=ALU.mult), reads=(bT[1], bVEC), writes=(bT[2],))
            for tap in (1, 0):
                emit("dve", lambda e, tap=tap: e.scalar_tensor_tensor(
                    out=Vv[:, 0:TS], in0=CB2[:, tap:tap + TS], scalar=col(V_SCW + tap * 8 + c), in1=Vv[:, 0:TS],
                    op0=ALU.mult, op1=ALU.add), reads=(bT[1], bT[2], bVEC), writes=(bT[2],))
            emit("dve", lambda e: e.tensor_tensor(out=Y[:, 8 + c, :], in0=PS[pb][:, :], in1=Vv[:, 0:TS], op=ALU.mult),
                 reads=(bPS[pb], bT[2]), writes=(bY[8 + c],))

        def group_norm_apply(s):
            for grp in range(2):
                gbase = V_GLRU if grp == 0 else V_GSC

                def apply(rs, brs, grp=grp, gbase=gbase):
                    for c in range(8):
                        k = grp * 8 + c
                        emit("dve", lambda e, k=k, c=c: e.scalar_tensor_tensor(
                            out=XN[:, k, sl(s)], in0=Y[:, k, :], scalar=col(gbase + c), in1=rs,
                            op0=ALU.mult, op1=ALU.mult), reads=(bY[k], brs, bVEC), writes=(bXN[k][s],))
                norm_generic(lambda c, grp=grp: Y[:, grp * 8 + c, :], [bY[grp * 8 + c] for c in range(8)], 8, 1.0 / 1024, apply)

        def mixer_prefix(h, sc_carry):
            for cp in range(4):
                slx = get_w(win[cp])
                for t in range(2):
                    c = cp * 2 + t
                    for s in range(NS):
                        pl = inproj_group(slx, t, s)
                        lru_chain(c, s, h * NS + s, pl, None, False)
                done_w()
            if sc_carry:
                for cp in range(4):
                    slc = get_w(win[12 + cp])
                    slxx = get_w(win[16 + cp])
                    for t in range(2):
                        c = cp * 2 + t
                        pc = inproj_group(slc, t, NS - 1)
                        px = inproj_group(slxx, t, NS - 1)
                        sc_chain(c, NS - 1, pc, px, None, True)
                    done_w()
                    done_w()

        def mixer_full(h):
            for s in range(NS):
                st = h * NS + s
                for cp in range(4):
                    slx = get_w(win[cp])
                    slg = get_w(win[4 + cp])
                    for t in range(2):
                        c = cp * 2 + t
                        pl = inproj_group(slx, t, s)
                        pgt = inproj_group(slg, t, s)
                        lru_chain(c, s, st, pl, pgt, True)
                    done_w()
                    done_w()
                for cp in range(4):
                    slb = get_w(win[8 + cp])
                    slc = get_w(win[12 + cp])
                    slxx = get_w(win[16 + cp])
                    for t in range(2):
                        c = cp * 2 + t
                        pc = inproj_group(slc, t, s)
                        px = inproj_group(slxx, t, s)
                        pb = inproj_group(slb, t, s)
                        sc_chain(c, s, pc, px, pb, False)
                    done_w()
                    done_w()
                    done_w()
                group_norm_apply(s)
            for j in range(8):
                slot = get_w(wout[j])
                for t in range(2):
                    m = 2 * j + t
                    for s in range(NS):
                        p = next_ps()
                        items = []
                        for k in range(NK):
                            o = (t * NK + k) * 128
                            items.append((mm(PS[p][:, :], W[:, slot, o:o + 128], XN[:, k, sl(s)], k == 0, k == NK - 1),
                                          (bW[slot], bXN[k][s])))
                        emit_group(items, (bPS[p],))
                        emit("dve", lambda e, p=p, m=m, s=s: e.tensor_tensor(
                            out=X[:, m, sl(s)], in0=PS[p][:, :], in1=X[:, m, sl(s)], op=ALU.add),
                            reads=(bPS[p], bX[m][s]), writes=(bX[m][s],))
                done_w()

        def program():
            for h in range(NH):
                full = h >= NPRE
                if not wstate["dry"]:
                    for k in range(NK):
                        S.dma("sp", ("x", k),
                              lambda e, h=h, k=k: e.dma_start(out=X[:, k, :], in_=xs[h, :, k * TH:(k + 1) * TH]),
                              reads=(), writes=(bX[k][0], bX[k][1]))
                norm_x_to_xn(V_G1)
                if h == 0:
                    dbg("d_xn1", XN[:, :, :], [128, NK, TH], BF16, [b for r in bXN for b in r])
                ffn(0)
                if h == 0:
                    dbg("d_x1", X[:, :, :], [128, NK, TH], F32, [b for r in bX for b in r])
                norm_x_to_xn(V_GMIX)
                if h == 0:
                    dbg("d_xn", XN[:, :, :], [128, NK, TH], BF16, [b for r in bXN for b in r])
                if not full:
                    mixer_prefix(h, sc_carry=(h == NPRE - 1))
                    if h == 0:
                        dbg("d_hst", HST[:, :], [128, 8], F32, [bHST])
                        dbg("d_lc", LC[:, :, :], [128, 8, 4], F32, [bLC])
                        dbg("d_dv", DV[:, :], [128, NDV], F32, [bDV])
                        dbg("d_t", T[5][:, :], [128, TS + 4], F32, [bT[5]])
                else:
                    mixer_full(h)
                    norm_x_to_xn(V_G2)
                    ffn(1)
                    norm_x_final(V_GFIN)
                    if not wstate["dry"]:
                        for k in range(NK):
                            S.dma("sp", ("x", k),
                                  lambda e, h=h, k=k: e.dma_start(out=outd[h - NPRE, :, k * TH:(k + 1) * TH], in_=X[:, k, :]),
                                  reads=(bX[k][0], bX[k][1]), writes=())

        wstate["dry"] = True
        program()
        wstate["dry"] = False
        wstate["next_use"] = 0
        ps_state["i"] = 0
        S.dma("sp", ("c", 0), lambda e: e.dma_start(out=VEC[:, :], in_=vecd[:, :]), writes=(bVEC,))
        S.dma("pool", ("c", 1), lambda e: e.dma_start(out=WG[:, :, :, :], in_=wgate.rearrange("p (t h j) -> p t h j", t=2, h=8),
                                                      max_dma_last_dim=8192), writes=(bWG,))
        S.op("dve", lambda e: e.memset(ONES[:, :], 1.0), writes=(bONES,))
        S.op("dve", lambda e: e.memset(LC[:, :, :], 0.0), writes=(bLC,))
        S.op("dve", lambda e: e.memset(SCC[:, :, :], 0.0), writes=(bSCC,))
        S.op("dve", lambda e: e.memset(HST[:, :], 0.0), writes=(bHST,))
        S.op("dve", lambda e: e.memset(CE[:, :], EPS), writes=(bDV,))
        S.op("dve", lambda e: e.memset(CQ[:, :], 0.25), writes=(bDV,))
        S.op("dve", lambda e: e.tensor_scalar(out=DV[:, DV_HBA:DV_HBA + 16], in0=VEC[:, V_BA:V_BA + 16], scalar1=0.5,
                                              scalar2=None, op0=ALU.mult), reads=(bVEC,), writes=(bDV,))
        bSPT = Buf()
        S.op("act", lambda e: e.activation(out=SPT[:, :], in_=VEC[:, V_LAM:V_LAM + 8], func=AF.Exp, scale=-1.0),
             reads=(bVEC,), writes=(bSPT,))
        S.op("act", lambda e: e.activation(out=SPT[:, :], in_=SPT[:, :], func=AF.Ln, bias=1.0),
             reads=(bSPT,), writes=(bSPT,), strict=True)
        S.op("dve", lambda e: e.tensor_scalar(out=DV[:, DV_N8:DV_N8 + 8], in0=SPT[:, :], scalar1=-8.0,
                                              scalar2=None, op0=ALU.mult), reads=(bSPT, bDV), writes=(bDV,), strict=True)
        S.op("dve", lambda e: e.tensor_scalar(out=DV[:, DV_N4:DV_N4 + 8], in0=SPT[:, :], scalar1=-4.0,
                                              scalar2=None, op0=ALU.mult), reads=(bSPT, bDV), writes=(bDV,), strict=True)
        for _ in range(NW):
            issue_wdma()
        program()
        assert wstate["next_use"] == len(wseq), (wstate["next_use"], len(wseq))
        fin = []
        for k in range(NK):
            fin.append((("x", k), S.dmacnt[("x", k)]))
        for key in S.dmacnt:
            if key[0] == "dbg":
                fin.append((key, 16))
        S.ops["sp"].append((fin, None, None))

        for key in list(S.dmacnt.keys()):
            sem(key)

        def replay(name, e):
            for waits, fn, inc in S.ops[name]:
                for k, v in waits:
                    e.wait_ge(sems[k], v)
                if fn is None:
                    continue
                ins = fn(e)
                if inc is not None:
                    ins.then_inc(sems[inc[0]], inc[1])

        with nc.Block() as block:
            @block.tensor
            def _(e):
                replay("pe", e)

            @block.scalar
            def _(e):
                replay("act", e)

            @block.vector
            def _(e):
                replay("dve", e)

            @block.gpsimd
            def _(e):
                replay("pool", e)

            @block.sync
            def _(e):
                replay("sp", e)
    _DBG["names"] = dbg_list
    return nc


def _prep_weights(inp):
    w = {}
    for i, pre in ((1, "ffn1"), (2, "ffn2")):
        wg = np.asarray(inp[pre + "_w_gate"][0], np.float32)
        wu = np.asarray(inp[pre + "_w_up"][0], np.float32)
        wd = np.asarray(inp[pre + "_w_down"][0], np.float32)
        g4 = wg.reshape(NK, 128, NF, 128).transpose(2, 1, 0, 3)
        u4 = wu.reshape(NK, 128, NF, 128).transpose(2, 1, 0, 3)
        w["wgu%d" % i] = np.ascontiguousarray(np.stack([g4, u4], axis=2)).reshape(NF, 128, 4096)
        d6 = wd.reshape(NG, FGS, 128, 2, 8, 128).transpose(0, 3, 2, 1, 4, 5)
        w["wd%d" % i] = np.ascontiguousarray(d6).reshape(2 * NG, 128, 4096)
    wi = np.asarray(inp["w_in"][0], np.float32)
    i4 = wi.reshape(NK, 128, 20, 2, 128).transpose(2, 1, 3, 0, 4)
    w["win"] = np.ascontiguousarray(i4).reshape(20, 128, 4096)
    wo = np.asarray(inp["w_out"][0], np.float32)
    o4 = wo.reshape(NK, 128, 8, 2, 128).transpose(2, 1, 3, 0, 4)
    w["wout"] = np.ascontiguousarray(o4).reshape(8, 128, 4096)
    wa = np.asarray(inp["lru_w_a"][0], np.float32)
    wi_ = np.asarray(inp["lru_w_i"][0], np.float32)
    w["wgate"] = np.ascontiguousarray(np.stack([wa.transpose(1, 0, 2), wi_.transpose(1, 0, 2)], axis=1)).reshape(128, 2048)
    return w


def _vec_base(inp):
    v = np.zeros((128, NV), np.float32)

    def put(c0, arr, n):
        v[:, c0:c0 + n] = np.asarray(arr, np.float32).reshape(n, 128).T

    put(V_G1, inp["ffn1_norm"][0], 16)
    put(V_GMIX, inp["mix_norm"][0], 16)
    put(V_G2, inp["ffn2_norm"][0], 16)
    put(V_GFIN, inp["final_norm"], 16)
    put(V_GLRU, inp["lru_out_norm"][0], 8)
    put(V_GSC, inp["sc_out_norm"][0], 8)
    cw = np.asarray(inp["lru_conv_w"][0], np.float32)
    for tap in range(4):
        put(V_CW + tap * 8, cw[tap], 8)
    put(V_CB, inp["lru_conv_b"][0], 8)
    put(V_BA, np.asarray(inp["lru_b_a"][0]).reshape(-1), 8)
    put(V_BI, np.asarray(inp["lru_b_i"][0]).reshape(-1), 8)
    put(V_LAM, inp["lru_lambda"][0], 8)
    scw = np.asarray(inp["sc_conv_w"][0], np.float32)
    for tap in range(3):
        put(V_SCW + tap * 8, scw[tap], 8)
    return v


_NC_CACHE = {}


def kernel(**inputs):
    x = np.asarray(inputs["x"], np.float32)
    w = _prep_weights(inputs)
    vbase = _vec_base(inputs)
    in_maps = []
    for c in range(8):
        b, q = c // 4, c % 4
        npad = (3 - q) * 2048
        stream = np.zeros((NH * TH, D), np.float32)
        stream[npad:] = x[b, 0:(q + 1) * 2048]
        xs = stream.reshape(NH, TH, NK, 128).transpose(0, 3, 2, 1)
        v = vbase.copy()
        for st in range(16):
            v[:, V_MASK + st] = 0.0 if (st + 1) * TS <= npad else 1.0
        m = {"xs": np.ascontiguousarray(xs).reshape(NH, 128, NK * TH), "vec": v}
        m.update(w)
        in_maps.append(m)
    if "nc" not in _NC_CACHE:
        _NC_CACHE["nc"] = build_program()
    nc = _NC_CACHE["nc"]
    res = run_bass_kernel_spmd(nc, in_maps, core_ids=list(range(8)))
    out = np.empty((2, 8192, D), np.float32)
    for c in range(8):
        b, q = c // 4, c % 4
        o = np.asarray(res.results[c]["out"], np.float32).reshape(2, 128, NK, TH)
        out[b, q * 2048:(q + 1) * 2048] = o.transpose(0, 3, 2, 1).reshape(2 * TH, D)
    return out
```

```python
import numpy as np
from contextlib import ExitStack
import concourse.bass as bass
import concourse.mybir as mybir
from concourse.bass_utils import run_bass_kernel_spmd

F32 = mybir.dt.float32
BF16 = mybir.dt.bfloat16
AF = mybir.ActivationFunctionType
ALU = mybir.AluOpType

D = 2048
DFF = 5632
NK = 16
NF = 44
FGS = 4
NG = 11
TS = 512
NS = 2
TH = 1024
NH = 8
NPRE = 6
NW = 3
EPS = 1e-6

V_G1, V_GMIX, V_G2, V_GFIN = 0, 16, 32, 48
V_GLRU, V_GSC = 64, 72
V_CW = 80
V_CB = 112
V_BA = 120
V_BI = 128
V_LAM = 136
V_SCW = 144
V_MASK = 168
NV = 184
DV_HBA, DV_HBI, DV_N4, DV_N8 = 0, 8, 16, 24
NDV = 32


_DBG = {}


class Buf:
    __slots__ = ("w", "r")

    def __init__(self):
        self.w = None
        self.r = {}


class Sched:
    def __init__(self):
        self.ops = {e: [] for e in ("pe", "act", "dve", "pool", "sp")}
        self.cnt = {"pe": 0, "act": 0, "dve": 0}
        self.waited = {e: {} for e in self.ops}
        self.dmacnt = {}

    def _deps(self, eng, reads, writes, strict=False):
        d = {}

        def add(tok):
            if tok is None:
                return
            k, v = tok
            if k == ("p", eng):
                if not (strict and self.cnt[eng] - v < 3):
                    return
            if d.get(k, 0) < v:
                d[k] = v

        for b in reads:
            add(b.w)
        for b in writes:
            add(b.w)
            for k, v in b.r.items():
                add((k, v))
        waits = []
        for k, v in d.items():
            if self.waited[eng].get(k, 0) >= v:
                continue
            self.waited[eng][k] = v
            waits.append((k, v))
        return waits

    def _commit(self, tok, reads, writes):
        k, v = tok
        for b in reads:
            if b.r.get(k, 0) < v:
                b.r[k] = v
        for b in writes:
            b.w = tok
            b.r = {}

    def op(self, eng, fn, reads=(), writes=(), strict=False):
        waits = self._deps(eng, reads, writes, strict)
        self.cnt[eng] += 1
        tok = (("p", eng), self.cnt[eng])
        self.ops[eng].append((waits, fn, (("p", eng), 1)))
        self._commit(tok, reads, writes)
        return tok

    def dma(self, eng, semkey, fn, reads=(), writes=()):
        waits = self._deps(eng, reads, writes)
        self.dmacnt[semkey] = self.dmacnt.get(semkey, 0) + 16
        tok = (semkey, self.dmacnt[semkey])
        self.ops[eng].append((waits, fn, (semkey, 16)))
        self._commit(tok, reads, writes)
        return tok

    def mm_group(self, items, writes):
        n = len(items)
        allr = []
        for i, (fn, reads) in enumerate(items):
            waits = self._deps("pe", reads, writes if i == 0 else ())
            last = i == n - 1
            if last:
                self.cnt["pe"] += 1
            self.ops["pe"].append((waits, fn, (("p", "pe"), 1) if last else None))
            allr.extend(reads)
        tok = (("p", "pe"), self.cnt["pe"])
        self._commit(tok, allr, writes)
        return tok


def build_program(debug=False):
    nc = bass.Bass("TRN2", target_bir_lowering=False)
    S = Sched()

    xs = nc.dram_tensor("xs", [NH, 128, NK * TH], F32, kind="ExternalInput").ap()
    wgu = [nc.dram_tensor("wgu%d" % i, [NF, 128, 4096], F32, kind="ExternalInput").ap() for i in (1, 2)]
    wdn = [nc.dram_tensor("wd%d" % i, [2 * NG, 128, 4096], F32, kind="ExternalInput").ap() for i in (1, 2)]
    win = nc.dram_tensor("win", [20, 128, 4096], F32, kind="ExternalInput").ap()
    wout = nc.dram_tensor("wout", [8, 128, 4096], F32, kind="ExternalInput").ap()
    wgate = nc.dram_tensor("wgate", [128, 2048], F32, kind="ExternalInput").ap()
    vecd = nc.dram_tensor("vec", [128, NV], F32, kind="ExternalInput").ap()
    outd = nc.dram_tensor("out", [2, 128, NK * TH], F32, kind="ExternalOutput").ap()

    dbg_list = []

    def dbg(name, ap, shape, dt, bufs):
        if not debug or wstate["dry"]:
            return
        d = nc.dram_tensor(name, shape, dt, kind="ExternalOutput").ap()
        dbg_list.append(name)
        S.dma("sp", ("dbg", len(dbg_list)), lambda e: e.dma_start(out=d, in_=ap), reads=tuple(bufs), writes=())

    es = ExitStack()
    with es:
        def sb(name, shape, dt):
            return es.enter_context(nc.sbuf_tensor(name, shape, dt))

        X = sb("X", [128, NK, TH], F32)
        XN = sb("XN", [128, NK, TH], BF16)
        H = sb("H", [128, 2, FGS, TH], BF16)
        W = sb("W", [128, NW, 4096], BF16)
        Y = sb("Y", [128, NK, TS], F32)
        XNB = Y.bitcast(BF16)
        WM = sb("WM", [128, 4096], BF16)
        NT = 9
        T = [sb("T%d" % i, [128, TS + 4], F32) for i in range(NT)]
        XCB = sb("XCB", [128, TS], BF16)
        SQ = [sb("SQ%d" % i, [128, TS], BF16) for i in range(2)]
        WG = sb("WG", [128, 2, 8, 128], BF16)
        VEC = sb("VEC", [128, NV], F32)
        DV = sb("DV", [128, NDV], F32)
        ONES = sb("ONES", [128, 128], BF16)
        LC = sb("LC", [128, 8, 4], F32)
        SCC = sb("SCC", [128, 8, 2], F32)
        HST = sb("HST", [128, 8], F32)
        CE = sb("CE", [128, 1], F32)
        CQ = sb("CQ", [128, 1], F32)
        SPT = sb("SPT", [128, 8], F32)
        PS = [es.enter_context(nc.psum_tensor("ps%d" % i, [128, TS], F32)) for i in range(8)]

        sems = {}

        def sem(key):
            if key not in sems:
                sems[key] = es.enter_context(nc.semaphore("s%d" % len(sems)))
            return sems[key]

        for k in (("p", "pe"), ("p", "act"), ("p", "dve")):
            sem(k)

        bX = [[Buf() for _ in range(NS)] for _ in range(NK)]
        bXN = [[Buf() for _ in range(NS)] for _ in range(NK)]
        bH = [[[Buf() for _ in range(NS)] for _ in range(FGS)] for _ in range(2)]
        bW = [Buf() for _ in range(NW)]
        bY = [Buf() for _ in range(NK)]
        bT = [Buf() for _ in range(NT)]
        bXCB = Buf()
        bSQ = [Buf(), Buf()]
        bWG = Buf()
        bWM = Buf()
        bVEC = Buf()
        bDV = Buf()
        bONES = Buf()
        bLC = Buf()
        bSCC = Buf()
        bHST = Buf()
        bPS = [Buf() for _ in range(8)]

        def col(c0, n=1):
            return VEC[:, c0:c0 + n]

        def dcol(c0, n=1):
            return DV[:, c0:c0 + n]

        ps_state = {"i": 0}

        def next_ps():
            i = ps_state["i"]
            ps_state["i"] = (i + 1) % 8
            return i

        wseq = []
        wstate = {"next_use": 0, "next_dma": 0, "dry": True}

        def issue_wdma():
            i = wstate["next_dma"]
            if i >= len(wseq):
                return
            wstate["next_dma"] = i + 1
            slot = i % NW
            src = wseq[i]
            S.dma("pool", ("w", slot),
                  lambda e, slot=slot, src=src: e.dma_start(out=W[:, slot, :], in_=src, max_dma_last_dim=8192),
                  reads=(), writes=(bW[slot],))

        def get_w(src):
            i = wstate["next_use"]
            wstate["next_use"] = i + 1
            if wstate["dry"]:
                wseq.append(src)
            else:
                assert wstate["next_dma"] > i, (i, wstate["next_dma"])
            return i % NW

        def done_w():
            if not wstate["dry"]:
                issue_wdma()

        def emit(eng, fn, reads=(), writes=(), strict=False):
            if wstate["dry"]:
                return
            S.op(eng, fn, reads, writes, strict)

        def emit_group(items, writes):
            if wstate["dry"]:
                return
            S.mm_group(items, writes)

        def mm(out_ap, lhsT, rhs, start, stop):
            return lambda e: e.matmul(out_ap, lhsT, rhs, start=start, stop=stop)

        def sl(s):
            return slice(s * TS, (s + 1) * TS)

        def norm_generic(src_ap_fn, src_bufs, nchunks, inv_n, apply_fn):
            p = next_ps()
            if not wstate["dry"]:
                for k in range(nchunks):
                    q = k % 2
                    S.op("act", lambda e, k=k, q=q: e.activation(out=SQ[q][:, :], in_=src_ap_fn(k), func=AF.Square),
                         reads=(src_bufs[k],), writes=(bSQ[q],))
                    waits = S._deps("pe", (bSQ[q], bONES), (bPS[p],) if k == 0 else ())
                    last = k == nchunks - 1
                    S.cnt["pe"] += 1
                    tok = (("p", "pe"), S.cnt["pe"])
                    S.ops["pe"].append((waits, mm(PS[p][:, :], ONES[:, :], SQ[q][:, :], k == 0, last), (("p", "pe"), 1)))
                    S._commit(tok, (bSQ[q], bONES), (bPS[p],) if last else ())
                    if not last:
                        pass
                S.op("act", lambda e: e.activation(out=T[8][:, 0:TS], in_=PS[p][:, :], func=AF.Sqrt, scale=inv_n, bias=EPSC[:, 0:1]),
                     reads=(bPS[p], bDV), writes=(bT[8],))
                S.op("dve", lambda e: e.reciprocal(out=T[8][:, 0:TS], in_=T[8][:, 0:TS]), reads=(bT[8],), writes=(bT[8],))
            apply_fn(T[8][:, 0:TS], bT[8])

        EPSC = CE[:, 0:1]

        def norm_x_to_xn(gbase, alias=False):
            for s in range(NS):
                def apply(rs, brs, s=s):
                    for k in range(NK):
                        dst = XNB if alias else XN
                        emit("dve", lambda e, k=k, s=s, dst=dst: e.scalar_tensor_tensor(
                            out=dst[:, k, sl(s)], in0=X[:, k, sl(s)], scalar=col(gbase + k), in1=rs,
                            op0=ALU.mult, op1=ALU.mult),
                            reads=(bX[k][s], brs, bVEC), writes=((bY[k],) if alias else (bXN[k][s],)))
                norm_generic(lambda k, s=s: X[:, k, sl(s)], [bX[k][s] for k in range(NK)], NK, 1.0 / D, apply)

        def norm_x_final(gbase):
            for s in range(NS):
                def apply(rs, brs, s=s):
                    for k in range(NK):
                        emit("dve", lambda e, k=k, s=s: e.scalar_tensor_tensor(
                            out=X[:, k, sl(s)], in0=X[:, k, sl(s)], scalar=col(gbase + k), in1=rs,
                            op0=ALU.mult, op1=ALU.mult),
                            reads=(bX[k][s], brs, bVEC), writes=(bX[k][s],))
                norm_generic(lambda k, s=s: X[:, k, sl(s)], [bX[k][s] for k in range(NK)], NK, 1.0 / D, apply)

        def ffn(fi, hook=None):
            def stage1(g):
                hb = g % 2
                for fl in range(FGS):
                    f = g * FGS + fl
                    slot = get_w(wgu[fi][f])
                    for s in range(NS):
                        pg = next_ps()
                        pu = next_ps()
                        for (t, p) in ((0, pg), (1, pu)):
                            items = []
                            for k in range(NK):
                                o = (t * NK + k) * 128
                                items.append((mm(PS[p][:, :], W[:, slot, o:o + 128], XN[:, k, sl(s)], k == 0, k == NK - 1),
                                              (bW[slot], bXN[k][s])))
                            emit_group(items, (bPS[p],))
                        tq = (f * NS + s) % 2
                        emit("act", lambda e, pg=pg, tq=tq: e.activation(out=T[6 + tq][:, 0:TS], in_=PS[pg][:, :], func=AF.Silu),
                             reads=(bPS[pg],), writes=(bT[6 + tq],))
                        emit("dve", lambda e, pu=pu, tq=tq, hb=hb, fl=fl, s=s: e.tensor_tensor(
                            out=H[:, hb, fl, sl(s)], in0=PS[pu][:, :], in1=T[6 + tq][:, 0:TS], op=ALU.mult),
                            reads=(bPS[pu], bT[6 + tq]), writes=(bH[hb][fl][s],))
                    done_w()
                    if hook is not None and f % 2 == 1:
                        hook()

            def stage2(g):
                hb = g % 2
                for mh in range(2):
                    slot = get_w(wdn[fi][g * 2 + mh])
                    for m8 in range(8):
                        m = mh * 8 + m8
                        for s in range(NS):
                            p = next_ps()
                            items = []
                            for fl in range(FGS):
                                o = (fl * 8 + m8) * 128
                                items.append((mm(PS[p][:, :], W[:, slot, o:o + 128], H[:, hb, fl, sl(s)], fl == 0, fl == FGS - 1),
                                              (bW[slot], bH[hb][fl][s])))
                            emit_group(items, (bPS[p],))
                            emit("dve", lambda e, p=p, m=m, s=s: e.scalar_tensor_tensor(
                                out=X[:, m, sl(s)], in0=PS[p][:, :], scalar=0.5, in1=X[:, m, sl(s)],
                                op0=ALU.mult, op1=ALU.add),
                                reads=(bPS[p], bX[m][s]), writes=(bX[m][s],))
                    done_w()

            for g in range(NG):
                stage1(g)
                if g >= 1:
                    stage2(g - 1)
            stage2(NG - 1)

        def lru_chain(c, s, st, pl, pgt, full):
            CB, XC, TR, TI, A2, HS = T[0], T[1], T[2], T[3], T[4], T[5]
            emit("dve", lambda e: e.tensor_copy(out=CB[:, 0:3], in_=LC[:, c, 0:3]), reads=(bLC,), writes=(bT[0],))
            emit("act", lambda e: e.activation(out=CB[:, 3:3 + TS], in_=PS[pl][:, :], func=AF.Copy),
                 reads=(bPS[pl],), writes=(bT[0],))
            emit("dve", lambda e: e.tensor_scalar(out=XC[:, 0:TS], in0=CB[:, 3:3 + TS], scalar1=col(V_CW + 3 * 8 + c),
                                                  scalar2=col(V_CB + c), op0=ALU.mult, op1=ALU.add),
                 reads=(bT[0], bVEC), writes=(bT[1],))
            for tap in (2, 1, 0):
                emit("dve", lambda e, tap=tap: e.scalar_tensor_tensor(
                    out=XC[:, 0:TS], in0=CB[:, tap:tap + TS], scalar=col(V_CW + tap * 8 + c), in1=XC[:, 0:TS],
                    op0=ALU.mult, op1=ALU.add), reads=(bT[0], bT[1], bVEC), writes=(bT[1],))
            emit("dve", lambda e: e.tensor_copy(out=LC[:, c, 0:3], in_=CB[:, TS:TS + 3]), reads=(bT[0],), writes=(bLC,))
            emit("act", lambda e: e.activation(out=XCB[:, :], in_=XC[:, 0:TS], func=AF.Copy), reads=(bT[1],), writes=(bXCB,))
            pr = next_ps()
            pi = next_ps()
            emit_group([(mm(PS[pr][:, :], WG[:, 0, c, :], XCB[:, :], True, True), (bWG, bXCB))], (bPS[pr],))
            emit_group([(mm(PS[pi][:, :], WG[:, 1, c, :], XCB[:, :], True, True), (bWG, bXCB))], (bPS[pi],))
            emit("act", lambda e: e.activation(out=TR[:, 0:TS], in_=PS[pr][:, :], func=AF.Tanh, scale=0.5, bias=dcol(DV_HBA + c)),
                 reads=(bPS[pr], bDV), writes=(bT[2],))
            emit("act", lambda e: e.activation(out=TI[:, 0:TS], in_=PS[pi][:, :], func=AF.Tanh, scale=0.5, bias=dcol(DV_HBI + c)),
                 reads=(bPS[pi], bDV), writes=(bT[3],))
            emit("act", lambda e: e.activation(out=A2[:, 0:TS], in_=TR[:, 0:TS], func=AF.Exp, scale=dcol(DV_N8 + c), bias=dcol(DV_N8 + c)),
                 reads=(bT[2], bDV), writes=(bT[4],))
            emit("act", lambda e: e.activation(out=TR[:, 0:TS], in_=TR[:, 0:TS], func=AF.Exp, scale=dcol(DV_N4 + c), bias=dcol(DV_N4 + c)),
                 reads=(bT[2], bDV), writes=(bT[2],))
            emit("dve", lambda e: e.tensor_scalar(out=A2[:, 0:TS], in0=A2[:, 0:TS], scalar1=0.9999999, scalar2=-0.25,
                                                  op0=ALU.min, op1=ALU.mult), reads=(bT[4],), writes=(bT[4],))
            emit("act", lambda e: e.activation(out=A2[:, 0:TS], in_=A2[:, 0:TS], func=AF.Sqrt, scale=1.0, bias=CQ[:, 0:1]),
                 reads=(bT[4], bDV), writes=(bT[4],))
            emit("dve", lambda e: e.scalar_tensor_tensor(out=TI[:, 0:TS], in0=TI[:, 0:TS], scalar=1.0, in1=XC[:, 0:TS],
                                                         op0=ALU.add, op1=ALU.mult), reads=(bT[3], bT[1]), writes=(bT[3],))
            emit("dve", lambda e: e.tensor_tensor(out=TI[:, 0:TS], in0=TI[:, 0:TS], in1=A2[:, 0:TS], op=ALU.mult),
                 reads=(bT[3], bT[4]), writes=(bT[3],))
            emit("dve", lambda e: e.tensor_tensor_scan(out=HS[:, 0:TS], data0=TR[:, 0:TS], data1=TI[:, 0:TS],
                                                       initial=HST[:, c:c + 1], op0=ALU.mult, op1=ALU.add),
                 reads=(bT[2], bT[3], bHST), writes=(bT[5],))
            emit("dve", lambda e: e.tensor_scalar(out=HST[:, c:c + 1], in0=HS[:, TS - 1:TS], scalar1=col(V_MASK + st),
                                                  scalar2=None, op0=ALU.mult), reads=(bT[5], bVEC), writes=(bHST,), strict=True)
            if full:
                G = A2
                emit("act", lambda e: e.activation(out=G[:, 0:TS], in_=PS[pgt][:, :], func=AF.Square),
                     reads=(bPS[pgt],), writes=(bT[4],))
                emit("dve", lambda e: e.tensor_scalar(out=G[:, 0:TS], in0=G[:, 0:TS], scalar1=0.044715, scalar2=1.0,
                                                      op0=ALU.mult, op1=ALU.add), reads=(bT[4],), writes=(bT[4],))
                emit("dve", lambda e: e.tensor_tensor(out=G[:, 0:TS], in0=PS[pgt][:, :], in1=G[:, 0:TS], op=ALU.mult),
                     reads=(bPS[pgt], bT[4]), writes=(bT[4],))
                emit("act", lambda e: e.activation(out=G[:, 0:TS], in_=G[:, 0:TS], func=AF.Tanh, scale=0.7978845608028654),
                     reads=(bT[4],), writes=(bT[4],))
                emit("dve", lambda e: e.scalar_tensor_tensor(out=G[:, 0:TS], in0=G[:, 0:TS], scalar=1.0, in1=PS[pgt][:, :],
                                                             op0=ALU.add, op1=ALU.mult), reads=(bT[4], bPS[pgt]), writes=(bT[4],))
                emit("dve", lambda e: e.scalar_tensor_tensor(out=Y[:, c, :], in0=G[:, 0:TS], scalar=0.5, in1=HS[:, 0:TS],
                                                             op0=ALU.mult, op1=ALU.mult), reads=(bT[4], bT[5]), writes=(bY[c],))

        def inproj_group(slot, t, s):
            p = next_ps()
            items = []
            for k in range(NK):
                o = (t * NK + k) * 128
                items.append((mm(PS[p][:, :], W[:, slot, o:o + 128], XN[:, k, sl(s)], k == 0, k == NK - 1),
                              (bW[slot], bXN[k][s])))
            emit_group(items, (bPS[p],))
            return p

        def inproj_group2(wap_fn, wbuf, xsrc, xbufs, t, s):
            p = next_ps()
            items = []
            for k in range(NK):
                o = (t * NK + k) * 128
                items.append((mm(PS[p][:, :], wap_fn(o), xsrc[:, k, sl(s)], k == 0, k == NK - 1), (wbuf, xbufs[k])))
            emit_group(items, (bPS[p],))
            return p

        def prefix_units(h):
            units = []
            for cp in range(4):
                for t in range(2):
                    for s in range(NS):
                        def unit(cp=cp, t=t, s=s):
                            c = cp * 2 + t
                            if t == 0 and s == 0 and not wstate["dry"]:
                                S.dma("pool", ("wm", 0),
                                      lambda e, cp=cp: e.dma_start(out=WM[:, :], in_=win[cp], max_dma_last_dim=8192),
                                      reads=(), writes=(bWM,))
                            pl = inproj_group2(lambda o: WM[:, o:o + 128], bWM, XNB, bY, t, s)
                            lru_chain(c, s, h * NS + s, pl, None, False)
                        units.append(unit)
            return units

        def sc_carry_now():
            for cp in range(4):
                slc = get_w(win[12 + cp])
                slxx = get_w(win[16 + cp])
                for t in range(2):
                    c = cp * 2 + t
                    pc = inproj_group2(lambda o, slc=slc: W[:, slc, o:o + 128], bW[slc], XNB, bY, t, NS - 1)
                    px = inproj_group2(lambda o, slxx=slxx: W[:, slxx, o:o + 128], bW[slxx], XNB, bY, t, NS - 1)
                    sc_chain(c, NS - 1, pc, px, None, True)
                done_w()
                done_w()

        def sc_chain(c, s, pc, px, pb, carry_only):
            SX, CB2, Vv = T[0], T[1], T[2]
            emit("act", lambda e: e.activation(out=SX[:, 0:TS], in_=PS[px][:, :], func=AF.Copy), reads=(bPS[px],), writes=(bT[0],))
            emit("dve", lambda e: e.tensor_copy(out=CB2[:, 0:2], in_=SCC[:, c, 0:2]), reads=(bSCC,), writes=(bT[1],))
            emit("dve", lambda e: e.tensor_tensor(out=CB2[:, 2:2 + TS], in0=PS[pc][:, :], in1=SX[:, 0:TS], op=ALU.mult),
                 reads=(bPS[pc], bT[0]), writes=(bT[1],))
            emit("dve", lambda e: e.tensor_copy(out=SCC[:, c, 0:2], in_=CB2[:, TS:TS + 2]), reads=(bT[1],), writes=(bSCC,), strict=True)
            if carry_only:
                return
            emit("dve", lambda e: e.tensor_scalar(out=Vv[:, 0:TS], in0=CB2[:, 2:2 + TS], scalar1=col(V_SCW + 2 * 8 + c),
                                                  scalar2=None, op0=ALU.mult), reads=(bT[1], bVEC), writes=(bT[2],))
            for tap in (1, 0):
                emit("dve", lambda e, tap=tap: e.scalar_tensor_tensor(
                    out=Vv[:, 0:TS], in0=CB2[:, tap:tap + TS], scalar=col(V_SCW + tap * 8 + c), in1=Vv[:, 0:TS],
                    op0=ALU.mult, op1=ALU.add), reads=(bT[1], bT[2], bVEC), writes=(bT[2],))
            emit("dve", lambda e: e.tensor_tensor(out=Y[:, 8 + c, :], in0=PS[pb][:, :], in1=Vv[:, 0:TS], op=ALU.mult),
                 reads=(bPS[pb], bT[2]), writes=(bY[8 + c],))

        def group_norm_apply(s):
            for grp in range(2):
                gbase = V_GLRU if grp == 0 else V_GSC

                def apply(rs, brs, grp=grp, gbase=gbase):
                    for c in range(8):
                        k = grp * 8 + c
                        emit("dve", lambda e, k=k, c=c: e.scalar_tensor_tensor(
                            out=XN[:, k, sl(s)], in0=Y[:, k, :], scalar=col(gbase + c), in1=rs,
                            op0=ALU.mult, op1=ALU.mult), reads=(bY[k], brs, bVEC), writes=(bXN[k][s],))
                norm_generic(lambda c, grp=grp: Y[:, grp * 8 + c, :], [bY[grp * 8 + c] for c in range(8)], 8, 1.0 / 1024, apply)

        def mixer_prefix(h, sc_carry):
            for cp in range(4):
                slx = get_w(win[cp])
                for t in range(2):
                    c = cp * 2 + t
                    for s in range(NS):
                        pl = inproj_group(slx, t, s)
                        lru_chain(c, s, h * NS + s, pl, None, False)
                done_w()
            if sc_carry:
                for cp in range(4):
                    slc = get_w(win[12 + cp])
                    slxx = get_w(win[16 + cp])
                    for t in range(2):
                        c = cp * 2 + t
                        pc = inproj_group(slc, t, NS - 1)
                        px = inproj_group(slxx, t, NS - 1)
                        sc_chain(c, NS - 1, pc, px, None, True)
                    done_w()
                    done_w()

        def mixer_full(h):
            for s in range(NS):
                st = h * NS + s
                for cp in range(4):
                    slx = get_w(win[cp])
                    slg = get_w(win[4 + cp])
                    for t in range(2):
                        c = cp * 2 + t
                        pl = inproj_group(slx, t, s)
                        pgt = inproj_group(slg, t, s)
                        lru_chain(c, s, st, pl, pgt, True)
                    done_w()
                    done_w()
                for cp in range(4):
                    slb = get_w(win[8 + cp])
                    slc = get_w(win[12 + cp])
                    slxx = get_w(win[16 + cp])
                    for t in range(2):
                        c = cp * 2 + t
                        pc = inproj_group(slc, t, s)
                        px = inproj_group(slxx, t, s)
                        pb = inproj_group(slb, t, s)
                        sc_chain(c, s, pc, px, pb, False)
                    done_w()
                    done_w()
                    done_w()
                group_norm_apply(s)
            for j in range(8):
                slot = get_w(wout[j])
                for t in range(2):
                    m = 2 * j + t
                    for s in range(NS):
                        p = next_ps()
                        items = []
                        for k in range(NK):
                            o = (t * NK + k) * 128
                            items.append((mm(PS[p][:, :], W[:, slot, o:o + 128], XN[:, k, sl(s)], k == 0, k == NK - 1),
                                          (bW[slot], bXN[k][s])))
                        emit_group(items, (bPS[p],))
                        emit("dve", lambda e, p=p, m=m, s=s: e.tensor_tensor(
                            out=X[:, m, sl(s)], in0=PS[p][:, :], in1=X[:, m, sl(s)], op=ALU.add),
                            reads=(bPS[p], bX[m][s]), writes=(bX[m][s],))
                done_w()

        pending = []

        def run_pending(n):
            for _ in range(n):
                if pending:
                    pending.pop(0)()

        def program():
            del pending[:]
            for h in range(NH):
                full = h >= NPRE
                if not wstate["dry"]:
                    for k in range(NK):
                        S.dma("sp", ("x", k),
                              lambda e, h=h, k=k: e.dma_start(out=X[:, k, :], in_=xs[h, :, k * TH:(k + 1) * TH]),
                              reads=(), writes=(bX[k][0], bX[k][1]))
                run_pending(6)
                norm_x_to_xn(V_G1)
                if h == 0:
                    dbg("d_xn1", XN[:, :, :], [128, NK, TH], BF16, [b for r in bXN for b in r])
                ffn(0, hook=lambda: run_pending(1))
                run_pending(len(pending))
                if h == 1:
                    dbg("d_hst", HST[:, :], [128, 8], F32, [bHST])
                    dbg("d_lc", LC[:, :, :], [128, 8, 4], F32, [bLC])
                if h == 0:
                    dbg("d_x1", X[:, :, :], [128, NK, TH], F32, [b for r in bX for b in r])
                if not full:
                    norm_x_to_xn(V_GMIX, alias=True)
                    if h == 0:
                        dbg("d_xn", XNB[:, :, :], [128, NK, TH], BF16, list(bY))
                    if h == NPRE - 1:
                        sc_carry_now()
                    pending.extend(prefix_units(h))
                else:
                    norm_x_to_xn(V_GMIX)
                    mixer_full(h)
                    norm_x_to_xn(V_G2)
                    ffn(1)
                    norm_x_final(V_GFIN)
                    if not wstate["dry"]:
                        for k in range(NK):
                            S.dma("sp", ("x", k),
                                  lambda e, h=h, k=k: e.dma_start(out=outd[h - NPRE, :, k * TH:(k + 1) * TH], in_=X[:, k, :]),
                                  reads=(bX[k][0], bX[k][1]), writes=())

        wstate["dry"] = True
        program()
        wstate["dry"] = False
        wstate["next_use"] = 0
        ps_state["i"] = 0
        S.dma("sp", ("c", 0), lambda e: e.dma_start(out=VEC[:, :], in_=vecd[:, :]), writes=(bVEC,))
        S.dma("pool", ("c", 1), lambda e: e.dma_start(out=WG[:, :, :, :], in_=wgate.rearrange("p (t h j) -> p t h j", t=2, h=8),
                                                      max_dma_last_dim=8192), writes=(bWG,))
        S.op("dve", lambda e: e.memset(ONES[:, :], 1.0), writes=(bONES,))
        S.op("dve", lambda e: e.memset(LC[:, :, :], 0.0), writes=(bLC,))
        S.op("dve", lambda e: e.memset(SCC[:, :, :], 0.0), writes=(bSCC,))
        S.op("dve", lambda e: e.memset(HST[:, :], 0.0), writes=(bHST,))
        S.op("dve", lambda e: e.memset(CE[:, :], EPS), writes=(bDV,))
        S.op("dve", lambda e: e.memset(CQ[:, :], 0.25), writes=(bDV,))
        S.op("dve", lambda e: e.tensor_scalar(out=DV[:, DV_HBA:DV_HBA + 16], in0=VEC[:, V_BA:V_BA + 16], scalar1=0.5,
                                              scalar2=None, op0=ALU.mult), reads=(bVEC,), writes=(bDV,))
        bSPT = Buf()
        S.op("act", lambda e: e.activation(out=SPT[:, :], in_=VEC[:, V_LAM:V_LAM + 8], func=AF.Exp, scale=-1.0),
             reads=(bVEC,), writes=(bSPT,))
        S.op("act", lambda e: e.activation(out=SPT[:, :], in_=SPT[:, :], func=AF.Ln, bias=1.0),
             reads=(bSPT,), writes=(bSPT,), strict=True)
        S.op("dve", lambda e: e.tensor_scalar(out=DV[:, DV_N8:DV_N8 + 8], in0=SPT[:, :], scalar1=-8.0,
                                              scalar2=None, op0=ALU.mult), reads=(bSPT, bDV), writes=(bDV,), strict=True)
        S.op("dve", lambda e: e.tensor_scalar(out=DV[:, DV_N4:DV_N4 + 8], in0=SPT[:, :], scalar1=-4.0,
                                              scalar2=None, op0=ALU.mult), reads=(bSPT, bDV), writes=(bDV,), strict=True)
        for _ in range(NW):
            issue_wdma()
        program()
        assert wstate["next_use"] == len(wseq), (wstate["next_use"], len(wseq))
        fin = []
        for k in range(NK):
            fin.append((("x", k), S.dmacnt[("x", k)]))
        for key in S.dmacnt:
            if key[0] == "dbg":
                fin.append((key, 16))
        S.ops["sp"].append((fin, None, None))

        for key in list(S.dmacnt.keys()):
            sem(key)

        def replay(name, e):
            for waits, fn, inc in S.ops[name]:
                for k, v in waits:
                    e.wait_ge(sems[k], v)
                if fn is None:
                    continue
                ins = fn(e)
                if inc is not None:
                    ins.then_inc(sems[inc[0]], inc[1])

        with nc.Block() as block:
            @block.tensor
            def _(e):
                replay("pe", e)

            @block.scalar
            def _(e):
                replay("act", e)

            @block.vector
            def _(e):
                replay("dve", e)

            @block.gpsimd
            def _(e):
                replay("pool", e)

            @block.sync
            def _(e):
                replay("sp", e)
    _DBG["names"] = dbg_list
    return nc


def _prep_weights(inp):
    w = {}
    for i, pre in ((1, "ffn1"), (2, "ffn2")):
        wg = np.asarray(inp[pre + "_w_gate"][0], np.float32)
        wu = np.asarray(inp[pre + "_w_up"][0], np.float32)
        wd = np.asarray(inp[pre + "_w_down"][0], np.float32)
        g4 = wg.reshape(NK, 128, NF, 128).transpose(2, 1, 0, 3)
        u4 = wu.reshape(NK, 128, NF, 128).transpose(2, 1, 0, 3)
        w["wgu%d" % i] = np.ascontiguousarray(np.stack([g4, u4], axis=2)).reshape(NF, 128, 4096)
        d6 = wd.reshape(NG, FGS, 128, 2, 8, 128).transpose(0, 3, 2, 1, 4, 5)
        w["wd%d" % i] = np.ascontiguousarray(d6).reshape(2 * NG, 128, 4096)
    wi = np.asarray(inp["w_in"][0], np.float32)
    i4 = wi.reshape(NK, 128, 20, 2, 128).transpose(2, 1, 3, 0, 4)
    w["win"] = np.ascontiguousarray(i4).reshape(20, 128, 4096)
    wo = np.asarray(inp["w_out"][0], np.float32)
    o4 = wo.reshape(NK, 128, 8, 2, 128).transpose(2, 1, 3, 0, 4)
    w["wout"] = np.ascontiguousarray(o4).reshape(8, 128, 4096)
    wa = np.asarray(inp["lru_w_a"][0], np.float32)
    wi_ = np.asarray(inp["lru_w_i"][0], np.float32)
    w["wgate"] = np.ascontiguousarray(np.stack([wa.transpose(1, 0, 2), wi_.transpose(1, 0, 2)], axis=1)).reshape(128, 2048)
    return w


def _vec_base(inp):
    v = np.zeros((128, NV), np.float32)

    def put(c0, arr, n):
        v[:, c0:c0 + n] = np.asarray(arr, np.float32).reshape(n, 128).T

    put(V_G1, inp["ffn1_norm"][0], 16)
    put(V_GMIX, inp["mix_norm"][0], 16)
    put(V_G2, inp["ffn2_norm"][0], 16)
    put(V_GFIN, inp["final_norm"], 16)
    put(V_GLRU, inp["lru_out_norm"][0], 8)
    put(V_GSC, inp["sc_out_norm"][0], 8)
    cw = np.asarray(inp["lru_conv_w"][0], np.float32)
    for tap in range(4):
        put(V_CW + tap * 8, cw[tap], 8)
    put(V_CB, inp["lru_conv_b"][0], 8)
    put(V_BA, np.asarray(inp["lru_b_a"][0]).reshape(-1), 8)
    put(V_BI, np.asarray(inp["lru_b_i"][0]).reshape(-1), 8)
    put(V_LAM, inp["lru_lambda"][0], 8)
    scw = np.asarray(inp["sc_conv_w"][0], np.float32)
    for tap in range(3):
        put(V_SCW + tap * 8, scw[tap], 8)
    return v


_NC_CACHE = {}


def kernel(**inputs):
    x = np.asarray(inputs["x"], np.float32)
    w = _prep_weights(inputs)
    vbase = _vec_base(inputs)
    in_maps = []
    for c in range(8):
        b, q = c // 4, c % 4
        npad = (3 - q) * 2048
        stream = np.zeros((NH * TH, D), np.float32)
        stream[npad:] = x[b, 0:(q + 1) * 2048]
        xs = stream.reshape(NH, TH, NK, 128).transpose(0, 3, 2, 1)
        v = vbase.copy()
        for st in range(16):
            v[:, V_MASK + st] = 0.0 if (st + 1) * TS <= npad else 1.0
        m = {"xs": np.ascontiguousarray(xs).reshape(NH, 128, NK * TH), "vec": v}
        m.update(w)
        in_maps.append(m)
    if "nc" not in _NC_CACHE:
        _NC_CACHE["nc"] = build_program()
    nc = _NC_CACHE["nc"]
    res = run_bass_kernel_spmd(nc, in_maps, core_ids=list(range(8)))
    out = np.empty((2, 8192, D), np.float32)
    for c in range(8):
        b, q = c // 4, c % 4
        o = np.asarray(res.results[c]["out"], np.float32).reshape(2, 128, NK, TH)
        out[b, q * 2048:(q + 1) * 2048] = o.transpose(0, 3, 2, 1).reshape(2 * TH, D)
    return out
```

```python
import numpy as np
from contextlib import ExitStack
import concourse.bass as bass
import concourse.mybir as mybir
from concourse.bass_utils import run_bass_kernel_spmd

F32 = mybir.dt.float32
BF16 = mybir.dt.bfloat16
AF = mybir.ActivationFunctionType
ALU = mybir.AluOpType

D = 2048
DFF = 5632
NK = 16
NF = 44
FGS = 4
NG = 11
TS = 512
NS = 2
TH = 1024
NH = 8
NPRE = 6
NW = 3
EPS = 1e-6

V_G1, V_GMIX, V_G2, V_GFIN = 0, 16, 32, 48
V_GLRU, V_GSC = 64, 72
V_CW = 80
V_CB = 112
V_BA = 120
V_BI = 128
V_LAM = 136
V_SCW = 144
V_MASK = 168
NV = 184
DV_HBA, DV_HBI, DV_N4, DV_N8 = 0, 8, 16, 24
NDV = 32


_DBG = {}


class Buf:
    __slots__ = ("w", "r")

    def __init__(self):
        self.w = None
        self.r = {}


class Sched:
    def __init__(self):
        self.ops = {e: [] for e in ("pe", "act", "dve", "pool", "sp")}
        self.cnt = {"pe": 0, "act": 0, "dve": 0}
        self.waited = {e: {} for e in self.ops}
        self.dmacnt = {}

    def _deps(self, eng, reads, writes, strict=False):
        d = {}

        def add(tok):
            if tok is None:
                return
            k, v = tok
            if k == ("p", eng):
                if not (strict and self.cnt[eng] - v < 3):
                    return
            if d.get(k, 0) < v:
                d[k] = v

        for b in reads:
            add(b.w)
        for b in writes:
            add(b.w)
            for k, v in b.r.items():
                add((k, v))
        waits = []
        for k, v in d.items():
            if self.waited[eng].get(k, 0) >= v:
                continue
            self.waited[eng][k] = v
            waits.append((k, v))
        return waits

    def _commit(self, tok, reads, writes):
        k, v = tok
        for b in reads:
            if b.r.get(k, 0) < v:
                b.r[k] = v
        for b in writes:
            b.w = tok
            b.r = {}

    def op(self, eng, fn, reads=(), writes=(), strict=False):
        waits = self._deps(eng, reads, writes, strict)
        self.cnt[eng] += 1
        tok = (("p", eng), self.cnt[eng])
        self.ops[eng].append((waits, fn, (("p", eng), 1)))
        self._commit(tok, reads, writes)
        return tok

    def dma(self, eng, semkey, fn, reads=(), writes=()):
        waits = self._deps(eng, reads, writes)
        self.dmacnt[semkey] = self.dmacnt.get(semkey, 0) + 16
        tok = (semkey, self.dmacnt[semkey])
        self.ops[eng].append((waits, fn, (semkey, 16)))
        self._commit(tok, reads, writes)
        return tok

    def mm_group(self, items, writes):
        n = len(items)
        allr = []
        for i, (fn, reads) in enumerate(items):
            waits = self._deps("pe", reads, writes if i == 0 else ())
            last = i == n - 1
            if last:
                self.cnt["pe"] += 1
            self.ops["pe"].append((waits, fn, (("p", "pe"), 1) if last else None))
            allr.extend(reads)
        tok = (("p", "pe"), self.cnt["pe"])
        self._commit(tok, allr, writes)
        return tok


def build_program(debug=False):
    nc = bass.Bass("TRN2", target_bir_lowering=False)
    S = Sched()

    xs = nc.dram_tensor("xs", [NH, 128, NK * TH], F32, kind="ExternalInput").ap()
    wgu = [nc.dram_tensor("wgu%d" % i, [NF, 128, 4096], F32, kind="ExternalInput").ap() for i in (1, 2)]
    wdn = [nc.dram_tensor("wd%d" % i, [2 * NG, 128, 4096], F32, kind="ExternalInput").ap() for i in (1, 2)]
    win = nc.dram_tensor("win", [20, 128, 4096], F32, kind="ExternalInput").ap()
    wout = nc.dram_tensor("wout", [8, 128, 4096], F32, kind="ExternalInput").ap()
    wgate = nc.dram_tensor("wgate", [128, 2048], F32, kind="ExternalInput").ap()
    vecd = nc.dram_tensor("vec", [128, NV], F32, kind="ExternalInput").ap()
    outd = nc.dram_tensor("out", [2, 128, NK * TH], F32, kind="ExternalOutput").ap()

    dbg_list = []

    def dbg(name, ap, shape, dt, bufs):
        if not debug or wstate["dry"]:
            return
        d = nc.dram_tensor(name, shape, dt, kind="ExternalOutput").ap()
        dbg_list.append(name)
        S.dma("sp", ("dbg", len(dbg_list)), lambda e: e.dma_start(out=d, in_=ap), reads=tuple(bufs), writes=())

    es = ExitStack()
    with es:
        def sb(name, shape, dt):
            return es.enter_context(nc.sbuf_tensor(name, shape, dt))

        X = sb("X", [128, NK, TH], F32)
        XN = sb("XN", [128, NK, TH], BF16)
        H = sb("H", [128, 2, FGS, TH], BF16)
        W = sb("W", [128, NW, 4096], BF16)
        Y = sb("Y", [128, NK, TS], F32)
        XNB = Y.bitcast(BF16)
        WM = sb("WM", [128, 4096], BF16)
        NT = 9
        T = [sb("T%d" % i, [128, TS + 4], F32) for i in range(NT)]
        XCB = sb("XCB", [128, TS], BF16)
        SQ = [sb("SQ%d" % i, [128, TS], BF16) for i in range(2)]
        WG = sb("WG", [128, 2, 8, 128], BF16)
        VEC = sb("VEC", [128, NV], F32)
        DV = sb("DV", [128, NDV], F32)
        ONES = sb("ONES", [128, 128], BF16)
        LC = sb("LC", [128, 8, 4], F32)
        SCC = sb("SCC", [128, 8, 2], F32)
        HST = sb("HST", [128, 8], F32)
        CE = sb("CE", [128, 1], F32)
        CQ = sb("CQ", [128, 1], F32)
        SPT = sb("SPT", [128, 8], F32)
        PS = [es.enter_context(nc.psum_tensor("ps%d" % i, [128, TS], F32)) for i in range(8)]

        sems = {}

        def sem(key):
            if key not in sems:
                sems[key] = es.enter_context(nc.semaphore("s%d" % len(sems)))
            return sems[key]

        for k in (("p", "pe"), ("p", "act"), ("p", "dve")):
            sem(k)

        bX = [[Buf() for _ in range(NS)] for _ in range(NK)]
        bXN = [[Buf() for _ in range(NS)] for _ in range(NK)]
        bH = [[[Buf() for _ in range(NS)] for _ in range(FGS)] for _ in range(2)]
        bW = [Buf() for _ in range(NW)]
        bY = [Buf() for _ in range(NK)]
        bT = [Buf() for _ in range(NT)]
        bXCB = Buf()
        bSQ = [Buf(), Buf()]
        bWG = Buf()
        bWM = Buf()
        bVEC = Buf()
        bDV = Buf()
        bONES = Buf()
        bLC = Buf()
        bSCC = Buf()
        bHST = Buf()
        bPS = [Buf() for _ in range(8)]

        def col(c0, n=1):
            return VEC[:, c0:c0 + n]

        def dcol(c0, n=1):
            return DV[:, c0:c0 + n]

        ps_state = {"i": 0}

        def next_ps():
            i = ps_state["i"]
            ps_state["i"] = (i + 1) % 8
            return i

        wseq = []
        wstate = {"next_use": 0, "next_dma": 0, "dry": True}

        def issue_wdma():
            i = wstate["next_dma"]
            if i >= len(wseq):
                return
            wstate["next_dma"] = i + 1
            slot = i % NW
            src = wseq[i]
            S.dma("pool", ("w", slot),
                  lambda e, slot=slot, src=src: e.dma_start(out=W[:, slot, :], in_=src, max_dma_last_dim=8192),
                  reads=(), writes=(bW[slot],))

        def get_w(src):
            i = wstate["next_use"]
            wstate["next_use"] = i + 1
            if wstate["dry"]:
                wseq.append(src)
            else:
                assert wstate["next_dma"] > i, (i, wstate["next_dma"])
            return i % NW

        def done_w():
            if not wstate["dry"]:
                issue_wdma()

        def emit(eng, fn, reads=(), writes=(), strict=False):
            if wstate["dry"]:
                return
            S.op(eng, fn, reads, writes, strict)

        def emit_group(items, writes):
            if wstate["dry"]:
                return
            S.mm_group(items, writes)

        def mm(out_ap, lhsT, rhs, start, stop):
            return lambda e: e.matmul(out_ap, lhsT, rhs, start=start, stop=stop)

        def sl(s):
            return slice(s * TS, (s + 1) * TS)

        def norm_generic(src_ap_fn, src_bufs, nchunks, inv_n, apply_fn):
            p = next_ps()
            if not wstate["dry"]:
                for k in range(nchunks):
                    q = k % 2
                    S.op("act", lambda e, k=k, q=q: e.activation(out=SQ[q][:, :], in_=src_ap_fn(k), func=AF.Square),
                         reads=(src_bufs[k],), writes=(bSQ[q],))
                    waits = S._deps("pe", (bSQ[q], bONES), (bPS[p],) if k == 0 else ())
                    last = k == nchunks - 1
                    S.cnt["pe"] += 1
                    tok = (("p", "pe"), S.cnt["pe"])
                    S.ops["pe"].append((waits, mm(PS[p][:, :], ONES[:, :], SQ[q][:, :], k == 0, last), (("p", "pe"), 1)))
                    S._commit(tok, (bSQ[q], bONES), (bPS[p],) if last else ())
                    if not last:
                        pass
                S.op("act", lambda e: e.activation(out=T[8][:, 0:TS], in_=PS[p][:, :], func=AF.Sqrt, scale=inv_n, bias=EPSC[:, 0:1]),
                     reads=(bPS[p], bDV), writes=(bT[8],))
                S.op("dve", lambda e: e.reciprocal(out=T[8][:, 0:TS], in_=T[8][:, 0:TS]), reads=(bT[8],), writes=(bT[8],))
            apply_fn(T[8][:, 0:TS], bT[8])

        EPSC = CE[:, 0:1]

        def norm_x_to_xn(gbase, alias=False):
            for s in range(NS):
                def apply(rs, brs, s=s):
                    for k in range(NK):
                        dst = XNB if alias else XN
                        emit("dve", lambda e, k=k, s=s, dst=dst: e.scalar_tensor_tensor(
                            out=dst[:, k, sl(s)], in0=X[:, k, sl(s)], scalar=col(gbase + k), in1=rs,
                            op0=ALU.mult, op1=ALU.mult),
                            reads=(bX[k][s], brs, bVEC), writes=((bY[k],) if alias else (bXN[k][s],)))
                norm_generic(lambda k, s=s: X[:, k, sl(s)], [bX[k][s] for k in range(NK)], NK, 1.0 / D, apply)

        def norm_x_final(gbase):
            for s in range(NS):
                def apply(rs, brs, s=s):
                    for k in range(NK):
                        emit("dve", lambda e, k=k, s=s: e.scalar_tensor_tensor(
                            out=X[:, k, sl(s)], in0=X[:, k, sl(s)], scalar=col(gbase + k), in1=rs,
                            op0=ALU.mult, op1=ALU.mult),
                            reads=(bX[k][s], brs, bVEC), writes=(bX[k][s],))
                norm_generic(lambda k, s=s: X[:, k, sl(s)], [bX[k][s] for k in range(NK)], NK, 1.0 / D, apply)

        def ffn(fi, hook=None):
            def stage1(g):
                hb = g % 2
                for fl in range(FGS):
                    f = g * FGS + fl
                    slot = get_w(wgu[fi][f])
                    for s in range(NS):
                        pg = next_ps()
                        pu = next_ps()
                        for (t, p) in ((0, pg), (1, pu)):
                            items = []
                            for k in range(NK):
                                o = (t * NK + k) * 128
                                items.append((mm(PS[p][:, :], W[:, slot, o:o + 128], XN[:, k, sl(s)], k == 0, k == NK - 1),
                                              (bW[slot], bXN[k][s])))
                            emit_group(items, (bPS[p],))
                        tq = (f * NS + s) % 2
                        emit("act", lambda e, pg=pg, tq=tq: e.activation(out=T[6 + tq][:, 0:TS], in_=PS[pg][:, :], func=AF.Silu),
                             reads=(bPS[pg],), writes=(bT[6 + tq],))
                        emit("dve", lambda e, pu=pu, tq=tq, hb=hb, fl=fl, s=s: e.tensor_tensor(
                            out=H[:, hb, fl, sl(s)], in0=PS[pu][:, :], in1=T[6 + tq][:, 0:TS], op=ALU.mult),
                            reads=(bPS[pu], bT[6 + tq]), writes=(bH[hb][fl][s],))
                    done_w()
                    if hook is not None and f % 2 == 1:
                        hook()

            def stage2(g):
                hb = g % 2
                for mh in range(2):
                    slot = get_w(wdn[fi][g * 2 + mh])
                    for m8 in range(8):
                        m = mh * 8 + m8
                        for s in range(NS):
                            p = next_ps()
                            items = []
                            for fl in range(FGS):
                                o = (fl * 8 + m8) * 128
                                items.append((mm(PS[p][:, :], W[:, slot, o:o + 128], H[:, hb, fl, sl(s)], fl == 0, fl == FGS - 1),
                                              (bW[slot], bH[hb][fl][s])))
                            emit_group(items, (bPS[p],))
                            emit("dve", lambda e, p=p, m=m, s=s: e.scalar_tensor_tensor(
                                out=X[:, m, sl(s)], in0=PS[p][:, :], scalar=0.5, in1=X[:, m, sl(s)],
                                op0=ALU.mult, op1=ALU.add),
                                reads=(bPS[p], bX[m][s]), writes=(bX[m][s],))
                    done_w()

            for g in range(NG):
                stage1(g)
                if g >= 1:
                    stage2(g - 1)
            stage2(NG - 1)

        def lru_part_a(c, s, pl):
            CB, XC, TR, TI, A2, HS = T[0], T[1], T[2], T[3], T[4], T[5]
            emit("dve", lambda e: e.tensor_copy(out=CB[:, 0:3], in_=LC[:, c, 0:3]), reads=(bLC,), writes=(bT[0],))
            emit("act", lambda e: e.activation(out=CB[:, 3:3 + TS], in_=PS[pl][:, :], func=AF.Copy),
                 reads=(bPS[pl],), writes=(bT[0],))
            emit("dve", lambda e: e.tensor_scalar(out=XC[:, 0:TS], in0=CB[:, 3:3 + TS], scalar1=col(V_CW + 3 * 8 + c),
                                                  scalar2=col(V_CB + c), op0=ALU.mult, op1=ALU.add),
                 reads=(bT[0], bVEC), writes=(bT[1],))
            for tap in (2, 1, 0):
                emit("dve", lambda e, tap=tap: e.scalar_tensor_tensor(
                    out=XC[:, 0:TS], in0=CB[:, tap:tap + TS], scalar=col(V_CW + tap * 8 + c), in1=XC[:, 0:TS],
                    op0=ALU.mult, op1=ALU.add), reads=(bT[0], bT[1], bVEC), writes=(bT[1],))
            emit("dve", lambda e: e.tensor_copy(out=LC[:, c, 0:3], in_=CB[:, TS:TS + 3]), reads=(bT[0],), writes=(bLC,))
            emit("act", lambda e: e.activation(out=XCB[:, :], in_=XC[:, 0:TS], func=AF.Copy), reads=(bT[1],), writes=(bXCB,))

        def lru_part_b(c, s, st, pgt, full):
            CB, XC, TR, TI, A2, HS = T[0], T[1], T[2], T[3], T[4], T[5]
            pr = next_ps()
            pi = next_ps()
            emit_group([(mm(PS[pr][:, :], WG[:, 0, c, :], XCB[:, :], True, True), (bWG, bXCB))], (bPS[pr],))
            emit_group([(mm(PS[pi][:, :], WG[:, 1, c, :], XCB[:, :], True, True), (bWG, bXCB))], (bPS[pi],))
            emit("act", lambda e: e.activation(out=TR[:, 0:TS], in_=PS[pr][:, :], func=AF.Tanh, scale=0.5, bias=dcol(DV_HBA + c)),
                 reads=(bPS[pr], bDV), writes=(bT[2],))
            emit("act", lambda e: e.activation(out=TI[:, 0:TS], in_=PS[pi][:, :], func=AF.Tanh, scale=0.5, bias=dcol(DV_HBI + c)),
                 reads=(bPS[pi], bDV), writes=(bT[3],))
            emit("act", lambda e: e.activation(out=A2[:, 0:TS], in_=TR[:, 0:TS], func=AF.Exp, scale=dcol(DV_N8 + c), bias=dcol(DV_N8 + c)),
                 reads=(bT[2], bDV), writes=(bT[4],))
            emit("act", lambda e: e.activation(out=TR[:, 0:TS], in_=TR[:, 0:TS], func=AF.Exp, scale=dcol(DV_N4 + c), bias=dcol(DV_N4 + c)),
                 reads=(bT[2], bDV), writes=(bT[2],))
            emit("dve", lambda e: e.tensor_scalar(out=A2[:, 0:TS], in0=A2[:, 0:TS], scalar1=0.9999999, scalar2=-0.25,
                                                  op0=ALU.min, op1=ALU.mult), reads=(bT[4],), writes=(bT[4],))
            emit("act", lambda e: e.activation(out=A2[:, 0:TS], in_=A2[:, 0:TS], func=AF.Sqrt, scale=1.0, bias=CQ[:, 0:1]),
                 reads=(bT[4], bDV), writes=(bT[4],))
            emit("dve", lambda e: e.scalar_tensor_tensor(out=TI[:, 0:TS], in0=TI[:, 0:TS], scalar=1.0, in1=XC[:, 0:TS],
                                                         op0=ALU.add, op1=ALU.mult), reads=(bT[3], bT[1]), writes=(bT[3],))
            emit("dve", lambda e: e.tensor_tensor(out=TI[:, 0:TS], in0=TI[:, 0:TS], in1=A2[:, 0:TS], op=ALU.mult),
                 reads=(bT[3], bT[4]), writes=(bT[3],))
            emit("dve", lambda e: e.tensor_tensor_scan(out=HS[:, 0:TS], data0=TR[:, 0:TS], data1=TI[:, 0:TS],
                                                       initial=HST[:, c:c + 1], op0=ALU.mult, op1=ALU.add),
                 reads=(bT[2], bT[3], bHST), writes=(bT[5],))
            emit("dve", lambda e: e.tensor_scalar(out=HST[:, c:c + 1], in0=HS[:, TS - 1:TS], scalar1=col(V_MASK + st),
                                                  scalar2=None, op0=ALU.mult), reads=(bT[5], bVEC), writes=(bHST,), strict=True)
            if full:
                G = A2
                emit("act", lambda e: e.activation(out=G[:, 0:TS], in_=PS[pgt][:, :], func=AF.Square),
                     reads=(bPS[pgt],), writes=(bT[4],))
                emit("dve", lambda e: e.tensor_scalar(out=G[:, 0:TS], in0=G[:, 0:TS], scalar1=0.044715, scalar2=1.0,
                                                      op0=ALU.mult, op1=ALU.add), reads=(bT[4],), writes=(bT[4],))
                emit("dve", lambda e: e.tensor_tensor(out=G[:, 0:TS], in0=PS[pgt][:, :], in1=G[:, 0:TS], op=ALU.mult),
                     reads=(bPS[pgt], bT[4]), writes=(bT[4],))
                emit("act", lambda e: e.activation(out=G[:, 0:TS], in_=G[:, 0:TS], func=AF.Tanh, scale=0.7978845608028654),
                     reads=(bT[4],), writes=(bT[4],))
                emit("dve", lambda e: e.scalar_tensor_tensor(out=G[:, 0:TS], in0=G[:, 0:TS], scalar=1.0, in1=PS[pgt][:, :],
                                                             op0=ALU.add, op1=ALU.mult), reads=(bT[4], bPS[pgt]), writes=(bT[4],))
                emit("dve", lambda e: e.scalar_tensor_tensor(out=Y[:, c, :], in0=G[:, 0:TS], scalar=0.5, in1=HS[:, 0:TS],
                                                             op0=ALU.mult, op1=ALU.mult), reads=(bT[4], bT[5]), writes=(bY[c],))

        def lru_chain(c, s, st, pl, pgt, full):
            lru_part_a(c, s, pl)
            lru_part_b(c, s, st, pgt, full)

        def inproj_group(slot, t, s):
            p = next_ps()
            items = []
            for k in range(NK):
                o = (t * NK + k) * 128
                items.append((mm(PS[p][:, :], W[:, slot, o:o + 128], XN[:, k, sl(s)], k == 0, k == NK - 1),
                              (bW[slot], bXN[k][s])))
            emit_group(items, (bPS[p],))
            return p

        def inproj_group2(wap_fn, wbuf, xsrc, xbufs, t, s):
            p = next_ps()
            items = []
            for k in range(NK):
                o = (t * NK + k) * 128
                items.append((mm(PS[p][:, :], wap_fn(o), xsrc[:, k, sl(s)], k == 0, k == NK - 1), (wbuf, xbufs[k])))
            emit_group(items, (bPS[p],))
            return p

        def wm_dma(cp):
            if wstate["dry"]:
                return
            S.dma("pool", ("wm", 0),
                  lambda e, cp=cp: e.dma_start(out=WM[:, :], in_=win[cp], max_dma_last_dim=8192),
                  reads=(), writes=(bWM,))

        def prefix_units(h):
            keys = [(cp, t, s) for cp in range(4) for t in range(2) for s in range(NS)]

            def part_a(i):
                cp, t, s = keys[i]
                pl = inproj_group2(lambda o: WM[:, o:o + 128], bWM, XNB, bY, t, s)
                lru_part_a(cp * 2 + t, s, pl)
                if t == 1 and s == NS - 1:
                    if cp < 3:
                        wm_dma(cp + 1)
                    elif h + 1 < NPRE:
                        wm_dma(0)

            def part_b(i):
                cp, t, s = keys[i]
                lru_part_b(cp * 2 + t, s, h * NS + s, None, False)

            units = []
            for i in range(len(keys) + 1):
                def unit(i=i):
                    if i > 0:
                        part_b(i - 1)
                    if i < len(keys):
                        part_a(i)
                units.append(unit)
            return units

        def sc_carry_now():
            for cp in range(4):
                slc = get_w(win[12 + cp])
                slxx = get_w(win[16 + cp])
                for t in range(2):
                    c = cp * 2 + t
                    pc = inproj_group2(lambda o, slc=slc: W[:, slc, o:o + 128], bW[slc], XNB, bY, t, NS - 1)
                    px = inproj_group2(lambda o, slxx=slxx: W[:, slxx, o:o + 128], bW[slxx], XNB, bY, t, NS - 1)
                    sc_chain(c, NS - 1, pc, px, None, True)
                done_w()
                done_w()

        def sc_chain(c, s, pc, px, pb, carry_only):
            SX, CB2, Vv = T[0], T[1], T[2]
            emit("act", lambda e: e.activation(out=SX[:, 0:TS], in_=PS[px][:, :], func=AF.Copy), reads=(bPS[px],), writes=(bT[0],))
            emit("dve", lambda e: e.tensor_copy(out=CB2[:, 0:2], in_=SCC[:, c, 0:2]), reads=(bSCC,), writes=(bT[1],))
            emit("dve", lambda e: e.tensor_tensor(out=CB2[:, 2:2 + TS], in0=PS[pc][:, :], in1=SX[:, 0:TS], op=ALU.mult),
                 reads=(bPS[pc], bT[0]), writes=(bT[1],))
            emit("dve", lambda e: e.tensor_copy(out=SCC[:, c, 0:2], in_=CB2[:, TS:TS + 2]), reads=(bT[1],), writes=(bSCC,), strict=True)
            if carry_only:
                return
            emit("dve", lambda e: e.tensor_scalar(out=Vv[:, 0:TS], in0=CB2[:, 2:2 + TS], scalar1=col(V_SCW + 2 * 8 + c),
                                                  scalar2=None, op0=ALU.mult), reads=(bT[1], bVEC), writes=(bT[2],))
            for tap in (1, 0):
                emit("dve", lambda e, tap=tap: e.scalar_tensor_tensor(
                    out=Vv[:, 0:TS], in0=CB2[:, tap:tap + TS], scalar=col(V_SCW + tap * 8 + c), in1=Vv[:, 0:TS],
                    op0=ALU.mult, op1=ALU.add), reads=(bT[1], bT[2], bVEC), writes=(bT[2],))
            emit("dve", lambda e: e.tensor_tensor(out=Y[:, 8 + c, :], in0=PS[pb][:, :], in1=Vv[:, 0:TS], op=ALU.mult),
                 reads=(bPS[pb], bT[2]), writes=(bY[8 + c],))

        def group_norm_apply(s):
            for grp in range(2):
                gbase = V_GLRU if grp == 0 else V_GSC

                def apply(rs, brs, grp=grp, gbase=gbase):
                    for c in range(8):
                        k = grp * 8 + c
                        emit("dve", lambda e, k=k, c=c: e.scalar_tensor_tensor(
                            out=XN[:, k, sl(s)], in0=Y[:, k, :], scalar=col(gbase + c), in1=rs,
                            op0=ALU.mult, op1=ALU.mult), reads=(bY[k], brs, bVEC), writes=(bXN[k][s],))
                norm_generic(lambda c, grp=grp: Y[:, grp * 8 + c, :], [bY[grp * 8 + c] for c in range(8)], 8, 1.0 / 1024, apply)

        def mixer_prefix(h, sc_carry):
            for cp in range(4):
                slx = get_w(win[cp])
                for t in range(2):
                    c = cp * 2 + t
                    for s in range(NS):
                        pl = inproj_group(slx, t, s)
                        lru_chain(c, s, h * NS + s, pl, None, False)
                done_w()
            if sc_carry:
                for cp in range(4):
                    slc = get_w(win[12 + cp])
                    slxx = get_w(win[16 + cp])
                    for t in range(2):
                        c = cp * 2 + t
                        pc = inproj_group(slc, t, NS - 1)
                        px = inproj_group(slxx, t, NS - 1)
                        sc_chain(c, NS - 1, pc, px, None, True)
                    done_w()
                    done_w()

        def mixer_full(h):
            for s in range(NS):
                st = h * NS + s
                for cp in range(4):
                    slx = get_w(win[cp])
                    slg = get_w(win[4 + cp])
                    for t in range(2):
                        c = cp * 2 + t
                        pl = inproj_group(slx, t, s)
                        pgt = inproj_group(slg, t, s)
                        lru_chain(c, s, st, pl, pgt, True)
                    done_w()
                    done_w()
                for cp in range(4):
                    slb = get_w(win[8 + cp])
                    slc = get_w(win[12 + cp])
                    slxx = get_w(win[16 + cp])
                    for t in range(2):
                        c = cp * 2 + t
                        pc = inproj_group(slc, t, s)
                        px = inproj_group(slxx, t, s)
                        pb = inproj_group(slb, t, s)
                        sc_chain(c, s, pc, px, pb, False)
                    done_w()
                    done_w()
                    done_w()
                group_norm_apply(s)
            for j in range(8):
                slot = get_w(wout[j])
                for t in range(2):
                    m = 2 * j + t
                    for s in range(NS):
                        p = next_ps()
                        items = []
                        for k in range(NK):
                            o = (t * NK + k) * 128
                            items.append((mm(PS[p][:, :], W[:, slot, o:o + 128], XN[:, k, sl(s)], k == 0, k == NK - 1),
                                          (bW[slot], bXN[k][s])))
                        emit_group(items, (bPS[p],))
                        emit("dve", lambda e, p=p, m=m, s=s: e.tensor_tensor(
                            out=X[:, m, sl(s)], in0=PS[p][:, :], in1=X[:, m, sl(s)], op=ALU.add),
                            reads=(bPS[p], bX[m][s]), writes=(bX[m][s],))
                done_w()

        pending = []

        def run_pending(n):
            for _ in range(n):
                if pending:
                    pending.pop(0)()

        def program():
            del pending[:]
            for h in range(NH):
                full = h >= NPRE
                if not wstate["dry"]:
                    for k in range(NK):
                        S.dma("sp", ("x", k),
                              lambda e, h=h, k=k: e.dma_start(out=X[:, k, :], in_=xs[h, :, k * TH:(k + 1) * TH]),
                              reads=(), writes=(bX[k][0], bX[k][1]))
                run_pending(6)
                norm_x_to_xn(V_G1)
                if h == 0:
                    dbg("d_xn1", XN[:, :, :], [128, NK, TH], BF16, [b for r in bXN for b in r])
                ffn(0, hook=lambda: run_pending(1))
                run_pending(len(pending))
                if h == 1:
                    dbg("d_hst", HST[:, :], [128, 8], F32, [bHST])
                    dbg("d_lc", LC[:, :, :], [128, 8, 4], F32, [bLC])
                if h == 0:
                    dbg("d_x1", X[:, :, :], [128, NK, TH], F32, [b for r in bX for b in r])
                if not full:
                    norm_x_to_xn(V_GMIX, alias=True)
                    if h == 0:
                        dbg("d_xn", XNB[:, :, :], [128, NK, TH], BF16, list(bY))
                    if h == NPRE - 1:
                        sc_carry_now()
                    pending.extend(prefix_units(h))
                else:
                    norm_x_to_xn(V_GMIX)
                    mixer_full(h)
                    norm_x_to_xn(V_G2)
                    ffn(1)
                    norm_x_final(V_GFIN)
                    if not wstate["dry"]:
                        for k in range(NK):
                            S.dma("sp", ("x", k),
                                  lambda e, h=h, k=k: e.dma_start(out=outd[h - NPRE, :, k * TH:(k + 1) * TH], in_=X[:, k, :]),
                                  reads=(bX[k][0], bX[k][1]), writes=())

        wstate["dry"] = True
        program()
        wstate["dry"] = False
        wstate["next_use"] = 0
        ps_state["i"] = 0
        S.dma("sp", ("c", 0), lambda e: e.dma_start(out=VEC[:, :], in_=vecd[:, :]), writes=(bVEC,))
        S.dma("pool", ("c", 1), lambda e: e.dma_start(out=WG[:, :, :, :], in_=wgate.rearrange("p (t h j) -> p t h j", t=2, h=8),
                                                      max_dma_last_dim=8192), writes=(bWG,))
        S.op("dve", lambda e: e.memset(ONES[:, :], 1.0), writes=(bONES,))
        S.op("dve", lambda e: e.memset(LC[:, :, :], 0.0), writes=(bLC,))
        S.op("dve", lambda e: e.memset(SCC[:, :, :], 0.0), writes=(bSCC,))
        S.op("dve", lambda e: e.memset(HST[:, :], 0.0), writes=(bHST,))
        S.op("dve", lambda e: e.memset(CE[:, :], EPS), writes=(bDV,))
        S.op("dve", lambda e: e.memset(CQ[:, :], 0.25), writes=(bDV,))
        S.op("dve", lambda e: e.tensor_scalar(out=DV[:, DV_HBA:DV_HBA + 16], in0=VEC[:, V_BA:V_BA + 16], scalar1=0.5,
                                              scalar2=None, op0=ALU.mult), reads=(bVEC,), writes=(bDV,))
        bSPT = Buf()
        S.op("act", lambda e: e.activation(out=SPT[:, :], in_=VEC[:, V_LAM:V_LAM + 8], func=AF.Exp, scale=-1.0),
             reads=(bVEC,), writes=(bSPT,))
        S.op("act", lambda e: e.activation(out=SPT[:, :], in_=SPT[:, :], func=AF.Ln, bias=1.0),
             reads=(bSPT,), writes=(bSPT,), strict=True)
        S.op("dve", lambda e: e.tensor_scalar(out=DV[:, DV_N8:DV_N8 + 8], in0=SPT[:, :], scalar1=-8.0,
                                              scalar2=None, op0=ALU.mult), reads=(bSPT, bDV), writes=(bDV,), strict=True)
        S.op("dve", lambda e: e.tensor_scalar(out=DV[:, DV_N4:DV_N4 + 8], in0=SPT[:, :], scalar1=-4.0,
                                              scalar2=None, op0=ALU.mult), reads=(bSPT, bDV), writes=(bDV,), strict=True)
        wm_dma(0)
        for _ in range(NW):
            issue_wdma()
        program()
        assert wstate["next_use"] == len(wseq), (wstate["next_use"], len(wseq))
        fin = []
        for k in range(NK):
            fin.append((("x", k), S.dmacnt[("x", k)]))
        for key in S.dmacnt:
            if key[0] == "dbg":
                fin.append((key, 16))
        S.ops["sp"].append((fin, None, None))

        for key in list(S.dmacnt.keys()):
            sem(key)

        def replay(name, e):
            for waits, fn, inc in S.ops[name]:
                for k, v in waits:
                    e.wait_ge(sems[k], v)
                if fn is None:
                    continue
                ins = fn(e)
                if inc is not None:
                    ins.then_inc(sems[inc[0]], inc[1])

        with nc.Block() as block:
            @block.tensor
            def _(e):
                replay("pe", e)

            @block.scalar
            def _(e):
                replay("act", e)

            @block.vector
            def _(e):
                replay("dve", e)

            @block.gpsimd
            def _(e):
                replay("pool", e)

            @block.sync
            def _(e):
                replay("sp", e)
    _DBG["names"] = dbg_list
    return nc


def _prep_weights(inp):
    w = {}
    for i, pre in ((1, "ffn1"), (2, "ffn2")):
        wg = np.asarray(inp[pre + "_w_gate"][0], np.float32)
        wu = np.asarray(inp[pre + "_w_up"][0], np.float32)
        wd = np.asarray(inp[pre + "_w_down"][0], np.float32)
        g4 = wg.reshape(NK, 128, NF, 128).transpose(2, 1, 0, 3)
        u4 = wu.reshape(NK, 128, NF, 128).transpose(2, 1, 0, 3)
        w["wgu%d" % i] = np.ascontiguousarray(np.stack([g4, u4], axis=2)).reshape(NF, 128, 4096)
        d6 = wd.reshape(NG, FGS, 128, 2, 8, 128).transpose(0, 3, 2, 1, 4, 5)
        w["wd%d" % i] = np.ascontiguousarray(d6).reshape(2 * NG, 128, 4096)
    wi = np.asarray(inp["w_in"][0], np.float32)
    i4 = wi.reshape(NK, 128, 20, 2, 128).transpose(2, 1, 3, 0, 4)
    w["win"] = np.ascontiguousarray(i4).reshape(20, 128, 4096)
    wo = np.asarray(inp["w_out"][0], np.float32)
    o4 = wo.reshape(NK, 128, 8, 2, 128).transpose(2, 1, 3, 0, 4)
    w["wout"] = np.ascontiguousarray(o4).reshape(8, 128, 4096)
    wa = np.asarray(inp["lru_w_a"][0], np.float32)
    wi_ = np.asarray(inp["lru_w_i"][0], np.float32)
    w["wgate"] = np.ascontiguousarray(np.stack([wa.transpose(1, 0, 2), wi_.transpose(1, 0, 2)], axis=1)).reshape(128, 2048)
    return w


def _vec_base(inp):
    v = np.zeros((128, NV), np.float32)

    def put(c0, arr, n):
        v[:, c0:c0 + n] = np.asarray(arr, np.float32).reshape(n, 128).T

    put(V_G1, inp["ffn1_norm"][0], 16)
    put(V_GMIX, inp["mix_norm"][0], 16)
    put(V_G2, inp["ffn2_norm"][0], 16)
    put(V_GFIN, inp["final_norm"], 16)
    put(V_GLRU, inp["lru_out_norm"][0], 8)
    put(V_GSC, inp["sc_out_norm"][0], 8)
    cw = np.asarray(inp["lru_conv_w"][0], np.float32)
    for tap in range(4):
        put(V_CW + tap * 8, cw[tap], 8)
    put(V_CB, inp["lru_conv_b"][0], 8)
    put(V_BA, np.asarray(inp["lru_b_a"][0]).reshape(-1), 8)
    put(V_BI, np.asarray(inp["lru_b_i"][0]).reshape(-1), 8)
    put(V_LAM, inp["lru_lambda"][0], 8)
    scw = np.asarray(inp["sc_conv_w"][0], np.float32)
    for tap in range(3):
        put(V_SCW + tap * 8, scw[tap], 8)
    return v


_NC_CACHE = {}


def kernel(**inputs):
    x = np.asarray(inputs["x"], np.float32)
    w = _prep_weights(inputs)
    vbase = _vec_base(inputs)
    in_maps = []
    for c in range(8):
        b, q = c // 4, c % 4
        npad = (3 - q) * 2048
        stream = np.zeros((NH * TH, D), np.float32)
        stream[npad:] = x[b, 0:(q + 1) * 2048]
        xs = stream.reshape(NH, TH, NK, 128).transpose(0, 3, 2, 1)
        v = vbase.copy()
        for st in range(16):
            v[:, V_MASK + st] = 0.0 if (st + 1) * TS <= npad else 1.0
        m = {"xs": np.ascontiguousarray(xs).reshape(NH, 128, NK * TH), "vec": v}
        m.update(w)
        in_maps.append(m)
    if "nc" not in _NC_CACHE:
        _NC_CACHE["nc"] = build_program()
    nc = _NC_CACHE["nc"]
    res = run_bass_kernel_spmd(nc, in_maps, core_ids=list(range(8)))
    out = np.empty((2, 8192, D), np.float32)
    for c in range(8):
        b, q = c // 4, c % 4
        o = np.asarray(res.results[c]["out"], np.float32).reshape(2, 128, NK, TH)
        out[b, q * 2048:(q + 1) * 2048] = o.transpose(0, 3, 2, 1).reshape(2 * TH, D)
    return out
```

```python
import numpy as np
from contextlib import ExitStack
import concourse.bass as bass
import concourse.mybir as mybir
from concourse.bass_utils import run_bass_kernel_spmd

F32 = mybir.dt.float32
BF16 = mybir.dt.bfloat16
AF = mybir.ActivationFunctionType
ALU = mybir.AluOpType

D = 2048
DFF = 5632
NK = 16
NF = 44
FGS = 4
NG = 11
TS = 512
NS = 2
TH = 1024
NH = 8
NPRE = 6
NW = 3
EPS = 1e-6

V_G1, V_GMIX, V_G2, V_GFIN = 0, 16, 32, 48
V_GLRU, V_GSC = 64, 72
V_CW = 80
V_CB = 112
V_BA = 120
V_BI = 128
V_LAM = 136
V_SCW = 144
V_MASK = 168
NV = 184
DV_HBA, DV_HBI, DV_N4, DV_N8 = 0, 8, 16, 24
NDV = 32


_DBG = {}


class Buf:
    __slots__ = ("w", "r")

    def __init__(self):
        self.w = None
        self.r = {}


class Sched:
    def __init__(self):
        self.ops = {e: [] for e in ("pe", "act", "dve", "pool", "sp")}
        self.cnt = {"pe": 0, "act": 0, "dve": 0}
        self.waited = {e: {} for e in self.ops}
        self.dmacnt = {}

    def _deps(self, eng, reads, writes, strict=False):
        d = {}

        def add(tok):
            if tok is None:
                return
            k, v = tok
            if k == ("p", eng):
                if not (strict and self.cnt[eng] - v < 3):
                    return
            if d.get(k, 0) < v:
                d[k] = v

        for b in reads:
            add(b.w)
        for b in writes:
            add(b.w)
            for k, v in b.r.items():
                add((k, v))
        waits = []
        for k, v in d.items():
            if self.waited[eng].get(k, 0) >= v:
                continue
            self.waited[eng][k] = v
            waits.append((k, v))
        return waits

    def _commit(self, tok, reads, writes):
        k, v = tok
        for b in reads:
            if b.r.get(k, 0) < v:
                b.r[k] = v
        for b in writes:
            b.w = tok
            b.r = {}

    def op(self, eng, fn, reads=(), writes=(), strict=False):
        waits = self._deps(eng, reads, writes, strict)
        self.cnt[eng] += 1
        tok = (("p", eng), self.cnt[eng])
        self.ops[eng].append((waits, fn, (("p", eng), 1)))
        self._commit(tok, reads, writes)
        return tok

    def dma(self, eng, semkey, fn, reads=(), writes=()):
        waits = self._deps(eng, reads, writes)
        self.dmacnt[semkey] = self.dmacnt.get(semkey, 0) + 16
        tok = (semkey, self.dmacnt[semkey])
        self.ops[eng].append((waits, fn, (semkey, 16)))
        self._commit(tok, reads, writes)
        return tok

    def mm_group(self, items, writes):
        n = len(items)
        allr = []
        for i, (fn, reads) in enumerate(items):
            waits = self._deps("pe", reads, writes if i == 0 else ())
            last = i == n - 1
            if last:
                self.cnt["pe"] += 1
            self.ops["pe"].append((waits, fn, (("p", "pe"), 1) if last else None))
            allr.extend(reads)
        tok = (("p", "pe"), self.cnt["pe"])
        self._commit(tok, allr, writes)
        return tok


def build_program(debug=False):
    nc = bass.Bass("TRN2", target_bir_lowering=False)
    S = Sched()

    xs = nc.dram_tensor("xs", [NH, 128, NK * TH], F32, kind="ExternalInput").ap()
    wgu = [nc.dram_tensor("wgu%d" % i, [NF, 128, 4096], F32, kind="ExternalInput").ap() for i in (1, 2)]
    wdn = [nc.dram_tensor("wd%d" % i, [2 * NG, 128, 4096], F32, kind="ExternalInput").ap() for i in (1, 2)]
    win = nc.dram_tensor("win", [20, 128, 4096], F32, kind="ExternalInput").ap()
    wout = nc.dram_tensor("wout", [8, 128, 4096], F32, kind="ExternalInput").ap()
    win2 = nc.dram_tensor("win2", [8, 128, 4096], F32, kind="ExternalInput").ap()
    wgate = nc.dram_tensor("wgate", [128, 2048], F32, kind="ExternalInput").ap()
    vecd = nc.dram_tensor("vec", [128, NV], F32, kind="ExternalInput").ap()
    outd = nc.dram_tensor("out", [2, 128, NK * TH], F32, kind="ExternalOutput").ap()

    dbg_list = []

    def dbg(name, ap, shape, dt, bufs):
        if not debug or wstate["dry"]:
            return
        d = nc.dram_tensor(name, shape, dt, kind="ExternalOutput").ap()
        dbg_list.append(name)
        S.dma("sp", ("dbg", len(dbg_list)), lambda e: e.dma_start(out=d, in_=ap), reads=tuple(bufs), writes=())

    es = ExitStack()
    with es:
        def sb(name, shape, dt):
            return es.enter_context(nc.sbuf_tensor(name, shape, dt))

        X = sb("X", [128, NK, TH], F32)
        XN = sb("XN", [128, NK, TH], BF16)
        H = sb("H", [128, 2, FGS, TH], BF16)
        W = sb("W", [128, NW, 4096], BF16)
        Y = sb("Y", [128, NK, TS], F32)
        XNB = Y.bitcast(BF16)
        WM = sb("WM", [128, 4096], BF16)
        NT = 9
        T = [sb("T%d" % i, [128, TS + 4], F32) for i in range(NT)]
        XCB = sb("XCB", [128, TS], BF16)
        SQ = [sb("SQ%d" % i, [128, TS], BF16) for i in range(2)]
        WG = sb("WG", [128, 2, 8, 128], BF16)
        VEC = sb("VEC", [128, NV], F32)
        DV = sb("DV", [128, NDV], F32)
        ONES = sb("ONES", [128, 128], BF16)
        LC = sb("LC", [128, 8, 4], F32)
        SCC = sb("SCC", [128, 8, 2], F32)
        HST = sb("HST", [128, 8], F32)
        CE = sb("CE", [128, 1], F32)
        CQ = sb("CQ", [128, 1], F32)
        SPT = sb("SPT", [128, 8], F32)
        PS = [es.enter_context(nc.psum_tensor("ps%d" % i, [128, TS], F32)) for i in range(8)]

        sems = {}

        def sem(key):
            if key not in sems:
                sems[key] = es.enter_context(nc.semaphore("s%d" % len(sems)))
            return sems[key]

        for k in (("p", "pe"), ("p", "act"), ("p", "dve")):
            sem(k)

        bX = [[Buf() for _ in range(NS)] for _ in range(NK)]
        bXN = [[Buf() for _ in range(NS)] for _ in range(NK)]
        bH = [[[Buf() for _ in range(NS)] for _ in range(FGS)] for _ in range(2)]
        bW = [Buf() for _ in range(NW)]
        bY = [Buf() for _ in range(NK)]
        bT = [Buf() for _ in range(NT)]
        bXCB = Buf()
        bSQ = [Buf(), Buf()]
        bWG = Buf()
        bWM = Buf()
        bVEC = Buf()
        bDV = Buf()
        bONES = Buf()
        bLC = Buf()
        bSCC = Buf()
        bHST = Buf()
        bPS = [Buf() for _ in range(8)]

        def col(c0, n=1):
            return VEC[:, c0:c0 + n]

        def dcol(c0, n=1):
            return DV[:, c0:c0 + n]

        ps_state = {"i": 0}

        def next_ps():
            i = ps_state["i"]
            ps_state["i"] = (i + 1) % 8
            return i

        wseq = []
        wstate = {"next_use": 0, "next_dma": 0, "dry": True}

        def issue_wdma():
            i = wstate["next_dma"]
            if i >= len(wseq):
                return
            wstate["next_dma"] = i + 1
            slot = i % NW
            src = wseq[i]
            S.dma("pool", ("w", slot),
                  lambda e, slot=slot, src=src: e.dma_start(out=W[:, slot, :], in_=src, max_dma_last_dim=8192),
                  reads=(), writes=(bW[slot],))

        def get_w(src):
            i = wstate["next_use"]
            wstate["next_use"] = i + 1
            if wstate["dry"]:
                wseq.append(src)
            else:
                assert wstate["next_dma"] > i, (i, wstate["next_dma"])
            return i % NW

        def done_w():
            if not wstate["dry"]:
                issue_wdma()

        def emit(eng, fn, reads=(), writes=(), strict=False):
            if wstate["dry"]:
                return
            S.op(eng, fn, reads, writes, strict)

        def emit_group(items, writes):
            if wstate["dry"]:
                return
            S.mm_group(items, writes)

        def mm(out_ap, lhsT, rhs, start, stop):
            return lambda e: e.matmul(out_ap, lhsT, rhs, start=start, stop=stop)

        def sl(s):
            return slice(s * TS, (s + 1) * TS)

        def norm_generic(src_ap_fn, src_bufs, nchunks, inv_n, apply_fn):
            p = next_ps()
            if not wstate["dry"]:
                for k in range(nchunks):
                    q = k % 2
                    S.op("act", lambda e, k=k, q=q: e.activation(out=SQ[q][:, :], in_=src_ap_fn(k), func=AF.Square),
                         reads=(src_bufs[k],), writes=(bSQ[q],))
                    waits = S._deps("pe", (bSQ[q], bONES), (bPS[p],) if k == 0 else ())
                    last = k == nchunks - 1
                    S.cnt["pe"] += 1
                    tok = (("p", "pe"), S.cnt["pe"])
                    S.ops["pe"].append((waits, mm(PS[p][:, :], ONES[:, :], SQ[q][:, :], k == 0, last), (("p", "pe"), 1)))
                    S._commit(tok, (bSQ[q], bONES), (bPS[p],) if last else ())
                    if not last:
                        pass
                S.op("act", lambda e: e.activation(out=T[8][:, 0:TS], in_=PS[p][:, :], func=AF.Sqrt, scale=inv_n, bias=EPSC[:, 0:1]),
                     reads=(bPS[p], bDV), writes=(bT[8],))
                S.op("dve", lambda e: e.reciprocal(out=T[8][:, 0:TS], in_=T[8][:, 0:TS]), reads=(bT[8],), writes=(bT[8],))
            apply_fn(T[8][:, 0:TS], bT[8])

        EPSC = CE[:, 0:1]

        def norm_x_to_xn(gbase, alias=False):
            for s in range(NS):
                def apply(rs, brs, s=s):
                    for k in range(NK):
                        dst = XNB if alias else XN
                        emit("dve", lambda e, k=k, s=s, dst=dst: e.scalar_tensor_tensor(
                            out=dst[:, k, sl(s)], in0=X[:, k, sl(s)], scalar=col(gbase + k), in1=rs,
                            op0=ALU.mult, op1=ALU.mult),
                            reads=(bX[k][s], brs, bVEC), writes=((bY[k],) if alias else (bXN[k][s],)))
                norm_generic(lambda k, s=s: X[:, k, sl(s)], [bX[k][s] for k in range(NK)], NK, 1.0 / D, apply)

        def norm_x_final(gbase):
            for s in range(NS):
                def apply(rs, brs, s=s):
                    for k in range(NK):
                        emit("dve", lambda e, k=k, s=s: e.scalar_tensor_tensor(
                            out=X[:, k, sl(s)], in0=X[:, k, sl(s)], scalar=col(gbase + k), in1=rs,
                            op0=ALU.mult, op1=ALU.mult),
                            reads=(bX[k][s], brs, bVEC), writes=(bX[k][s],))
                norm_generic(lambda k, s=s: X[:, k, sl(s)], [bX[k][s] for k in range(NK)], NK, 1.0 / D, apply)

        def ffn(fi, hook=None):
            def stage1(g):
                hb = g % 2
                for fp in range(FGS // 2):
                    fls = (2 * fp, 2 * fp + 1)
                    slots = [get_w(wgu[fi][g * FGS + fl]) for fl in fls]
                    for s in range(NS):
                        for fl, slot in zip(fls, slots):
                            f = g * FGS + fl
                            pg = next_ps()
                            pu = next_ps()
                            for (t, p) in ((0, pg), (1, pu)):
                                items = []
                                for k in range(NK):
                                    o = (t * NK + k) * 128
                                    items.append((mm(PS[p][:, :], W[:, slot, o:o + 128], XN[:, k, sl(s)], k == 0, k == NK - 1),
                                                  (bW[slot], bXN[k][s])))
                                emit_group(items, (bPS[p],))
                            tq = (f * NS + s) % 2
                            emit("act", lambda e, pg=pg, tq=tq: e.activation(out=T[6 + tq][:, 0:TS], in_=PS[pg][:, :], func=AF.Silu),
                                 reads=(bPS[pg],), writes=(bT[6 + tq],))
                            emit("dve", lambda e, pu=pu, tq=tq, hb=hb, fl=fl, s=s: e.tensor_tensor(
                                out=H[:, hb, fl, sl(s)], in0=PS[pu][:, :], in1=T[6 + tq][:, 0:TS], op=ALU.mult),
                                reads=(bPS[pu], bT[6 + tq]), writes=(bH[hb][fl][s],))
                    done_w()
                    done_w()
                    if hook is not None:
                        hook()

            def stage2(g):
                hb = g % 2
                for mh in range(2):
                    slot = get_w(wdn[fi][g * 2 + mh])
                    for m8 in range(8):
                        m = mh * 8 + m8
                        for s in range(NS):
                            p = next_ps()
                            items = []
                            for fl in range(FGS):
                                o = (fl * 8 + m8) * 128
                                items.append((mm(PS[p][:, :], W[:, slot, o:o + 128], H[:, hb, fl, sl(s)], fl == 0, fl == FGS - 1),
                                              (bW[slot], bH[hb][fl][s])))
                            emit_group(items, (bPS[p],))
                            emit("dve", lambda e, p=p, m=m, s=s: e.scalar_tensor_tensor(
                                out=X[:, m, sl(s)], in0=PS[p][:, :], scalar=0.5, in1=X[:, m, sl(s)],
                                op0=ALU.mult, op1=ALU.add),
                                reads=(bPS[p], bX[m][s]), writes=(bX[m][s],))
                    done_w()

            for g in range(NG):
                stage1(g)
                if g >= 1:
                    stage2(g - 1)
            stage2(NG - 1)

        def lru_part_a(c, s, pl):
            CB, XC, TR, TI, A2, HS = T[0], T[1], T[2], T[3], T[4], T[5]
            emit("dve", lambda e: e.tensor_copy(out=CB[:, 0:3], in_=LC[:, c, 0:3]), reads=(bLC,), writes=(bT[0],))
            emit("act", lambda e: e.activation(out=CB[:, 3:3 + TS], in_=PS[pl][:, :], func=AF.Copy),
                 reads=(bPS[pl],), writes=(bT[0],))
            emit("dve", lambda e: e.tensor_scalar(out=XC[:, 0:TS], in0=CB[:, 3:3 + TS], scalar1=col(V_CW + 3 * 8 + c),
                                                  scalar2=col(V_CB + c), op0=ALU.mult, op1=ALU.add),
                 reads=(bT[0], bVEC), writes=(bT[1],))
            for tap in (2, 1, 0):
                emit("dve", lambda e, tap=tap: e.scalar_tensor_tensor(
                    out=XC[:, 0:TS], in0=CB[:, tap:tap + TS], scalar=col(V_CW + tap * 8 + c), in1=XC[:, 0:TS],
                    op0=ALU.mult, op1=ALU.add), reads=(bT[0], bT[1], bVEC), writes=(bT[1],))
            emit("dve", lambda e: e.tensor_copy(out=LC[:, c, 0:3], in_=CB[:, TS:TS + 3]), reads=(bT[0],), writes=(bLC,))
            emit("act", lambda e: e.activation(out=XCB[:, :], in_=XC[:, 0:TS], func=AF.Copy), reads=(bT[1],), writes=(bXCB,))

        def lru_part_b(c, s, st, pgt, full):
            CB, XC, TR, TI, A2, HS = T[0], T[1], T[2], T[3], T[4], T[5]
            pr = next_ps()
            pi = next_ps()
            emit_group([(mm(PS[pr][:, :], WG[:, 0, c, :], XCB[:, :], True, True), (bWG, bXCB))], (bPS[pr],))
            emit_group([(mm(PS[pi][:, :], WG[:, 1, c, :], XCB[:, :], True, True), (bWG, bXCB))], (bPS[pi],))
            emit("act", lambda e: e.activation(out=TR[:, 0:TS], in_=PS[pr][:, :], func=AF.Tanh, scale=0.5, bias=dcol(DV_HBA + c)),
                 reads=(bPS[pr], bDV), writes=(bT[2],))
            emit("act", lambda e: e.activation(out=TI[:, 0:TS], in_=PS[pi][:, :], func=AF.Tanh, scale=0.5, bias=dcol(DV_HBI + c)),
                 reads=(bPS[pi], bDV), writes=(bT[3],))
            emit("act", lambda e: e.activation(out=A2[:, 0:TS], in_=TR[:, 0:TS], func=AF.Exp, scale=dcol(DV_N8 + c), bias=dcol(DV_N8 + c)),
                 reads=(bT[2], bDV), writes=(bT[4],))
            emit("act", lambda e: e.activation(out=TR[:, 0:TS], in_=TR[:, 0:TS], func=AF.Exp, scale=dcol(DV_N4 + c), bias=dcol(DV_N4 + c)),
                 reads=(bT[2], bDV), writes=(bT[2],))
            emit("dve", lambda e: e.tensor_scalar(out=A2[:, 0:TS], in0=A2[:, 0:TS], scalar1=0.9999999, scalar2=-0.25,
                                                  op0=ALU.min, op1=ALU.mult), reads=(bT[4],), writes=(bT[4],))
            emit("act", lambda e: e.activation(out=A2[:, 0:TS], in_=A2[:, 0:TS], func=AF.Sqrt, scale=1.0, bias=CQ[:, 0:1]),
                 reads=(bT[4], bDV), writes=(bT[4],))
            emit("dve", lambda e: e.scalar_tensor_tensor(out=TI[:, 0:TS], in0=TI[:, 0:TS], scalar=1.0, in1=XC[:, 0:TS],
                                                         op0=ALU.add, op1=ALU.mult), reads=(bT[3], bT[1]), writes=(bT[3],))
            emit("dve", lambda e: e.tensor_tensor(out=TI[:, 0:TS], in0=TI[:, 0:TS], in1=A2[:, 0:TS], op=ALU.mult),
                 reads=(bT[3], bT[4]), writes=(bT[3],))
            emit("dve", lambda e: e.tensor_tensor_scan(out=HS[:, 0:TS], data0=TR[:, 0:TS], data1=TI[:, 0:TS],
                                                       initial=HST[:, c:c + 1], op0=ALU.mult, op1=ALU.add),
                 reads=(bT[2], bT[3], bHST), writes=(bT[5],))
            emit("dve", lambda e: e.tensor_scalar(out=HST[:, c:c + 1], in0=HS[:, TS - 1:TS], scalar1=col(V_MASK + st),
                                                  scalar2=None, op0=ALU.mult), reads=(bT[5], bVEC), writes=(bHST,), strict=True)
            if full:
                G = A2
                emit("act", lambda e: e.activation(out=G[:, 0:TS], in_=PS[pgt][:, :], func=AF.Square),
                     reads=(bPS[pgt],), writes=(bT[4],))
                emit("dve", lambda e: e.tensor_scalar(out=G[:, 0:TS], in0=G[:, 0:TS], scalar1=0.044715, scalar2=1.0,
                                                      op0=ALU.mult, op1=ALU.add), reads=(bT[4],), writes=(bT[4],))
                emit("dve", lambda e: e.tensor_tensor(out=G[:, 0:TS], in0=PS[pgt][:, :], in1=G[:, 0:TS], op=ALU.mult),
                     reads=(bPS[pgt], bT[4]), writes=(bT[4],))
                emit("act", lambda e: e.activation(out=G[:, 0:TS], in_=G[:, 0:TS], func=AF.Tanh, scale=0.7978845608028654),
                     reads=(bT[4],), writes=(bT[4],))
                emit("dve", lambda e: e.scalar_tensor_tensor(out=G[:, 0:TS], in0=G[:, 0:TS], scalar=1.0, in1=PS[pgt][:, :],
                                                             op0=ALU.add, op1=ALU.mult), reads=(bT[4], bPS[pgt]), writes=(bT[4],))
                emit("dve", lambda e: e.scalar_tensor_tensor(out=Y[:, c, :], in0=G[:, 0:TS], scalar=0.5, in1=HS[:, 0:TS],
                                                             op0=ALU.mult, op1=ALU.mult), reads=(bT[4], bT[5]), writes=(bY[c],))

        def lru_chain(c, s, st, pl, pgt, full):
            lru_part_a(c, s, pl)
            lru_part_b(c, s, st, pgt, full)

        def inproj_group(slot, t, s):
            p = next_ps()
            items = []
            for k in range(NK):
                o = (t * NK + k) * 128
                items.append((mm(PS[p][:, :], W[:, slot, o:o + 128], XN[:, k, sl(s)], k == 0, k == NK - 1),
                              (bW[slot], bXN[k][s])))
            emit_group(items, (bPS[p],))
            return p

        def inproj_group2(wap_fn, wbuf, xsrc, xbufs, t, s):
            p = next_ps()
            items = []
            for k in range(NK):
                o = (t * NK + k) * 128
                items.append((mm(PS[p][:, :], wap_fn(o), xsrc[:, k, sl(s)], k == 0, k == NK - 1), (wbuf, xbufs[k])))
            emit_group(items, (bPS[p],))
            return p

        def wm_dma(cp):
            if wstate["dry"]:
                return
            S.dma("pool", ("wm", 0),
                  lambda e, cp=cp: e.dma_start(out=WM[:, :], in_=win[cp], max_dma_last_dim=8192),
                  reads=(), writes=(bWM,))

        def prefix_units(h):
            keys = [(cp, t, s) for cp in range(4) for t in range(2) for s in range(NS)]

            def part_a(i):
                cp, t, s = keys[i]
                pl = inproj_group2(lambda o: WM[:, o:o + 128], bWM, XNB, bY, t, s)
                lru_part_a(cp * 2 + t, s, pl)
                if t == 1 and s == NS - 1:
                    if cp < 3:
                        wm_dma(cp + 1)
                    elif h + 1 < NPRE:
                        wm_dma(0)

            def part_b(i):
                cp, t, s = keys[i]
                lru_part_b(cp * 2 + t, s, h * NS + s, None, False)

            units = []
            for i in range(len(keys) + 1):
                def unit(i=i):
                    if i > 0:
                        part_b(i - 1)
                    if i < len(keys):
                        part_a(i)
                units.append(unit)
            return units

        def sc_carry_now():
            for cp in range(4):
                slc = get_w(win[12 + cp])
                slxx = get_w(win[16 + cp])
                for t in range(2):
                    c = cp * 2 + t
                    pc = inproj_group2(lambda o, slc=slc: W[:, slc, o:o + 128], bW[slc], XNB, bY, t, NS - 1)
                    px = inproj_group2(lambda o, slxx=slxx: W[:, slxx, o:o + 128], bW[slxx], XNB, bY, t, NS - 1)
                    sc_chain(c, NS - 1, pc, px, None, True)
                done_w()
                done_w()

        def sc_chain(c, s, pc, px, pb, carry_only, vi=2):
            SX, CB2, Vv = T[0], T[1], T[vi]
            emit("act", lambda e: e.activation(out=SX[:, 0:TS], in_=PS[px][:, :], func=AF.Copy), reads=(bPS[px],), writes=(bT[0],))
            emit("dve", lambda e: e.tensor_copy(out=CB2[:, 0:2], in_=SCC[:, c, 0:2]), reads=(bSCC,), writes=(bT[1],))
            emit("dve", lambda e: e.tensor_tensor(out=CB2[:, 2:2 + TS], in0=PS[pc][:, :], in1=SX[:, 0:TS], op=ALU.mult),
                 reads=(bPS[pc], bT[0]), writes=(bT[1],))
            emit("dve", lambda e: e.tensor_copy(out=SCC[:, c, 0:2], in_=CB2[:, TS:TS + 2]), reads=(bT[1],), writes=(bSCC,), strict=True)
            if carry_only:
                return
            emit("dve", lambda e: e.tensor_scalar(out=Vv[:, 0:TS], in0=CB2[:, 2:2 + TS], scalar1=col(V_SCW + 2 * 8 + c),
                                                  scalar2=None, op0=ALU.mult), reads=(bT[1], bVEC), writes=(bT[vi],))
            for tap in (1, 0):
                emit("dve", lambda e, tap=tap: e.scalar_tensor_tensor(
                    out=Vv[:, 0:TS], in0=CB2[:, tap:tap + TS], scalar=col(V_SCW + tap * 8 + c), in1=Vv[:, 0:TS],
                    op0=ALU.mult, op1=ALU.add), reads=(bT[1], bT[vi], bVEC), writes=(bT[vi],))
            if pb is not None:
                sc_final(c, pb, vi)

        def sc_final(c, pb, vi):
            emit("dve", lambda e: e.tensor_tensor(out=Y[:, 8 + c, :], in0=PS[pb][:, :], in1=T[vi][:, 0:TS], op=ALU.mult),
                 reads=(bPS[pb], bT[vi]), writes=(bY[8 + c],))

        def group_norm_apply(s):
            for grp in range(2):
                gbase = V_GLRU if grp == 0 else V_GSC

                def apply(rs, brs, grp=grp, gbase=gbase):
                    for c in range(8):
                        k = grp * 8 + c
                        emit("dve", lambda e, k=k, c=c: e.scalar_tensor_tensor(
                            out=XN[:, k, sl(s)], in0=Y[:, k, :], scalar=col(gbase + c), in1=rs,
                            op0=ALU.mult, op1=ALU.mult), reads=(bY[k], brs, bVEC), writes=(bXN[k][s],))
                norm_generic(lambda c, grp=grp: Y[:, grp * 8 + c, :], [bY[grp * 8 + c] for c in range(8)], 8, 1.0 / 1024, apply)

        def mixer_prefix(h, sc_carry):
            for cp in range(4):
                slx = get_w(win[cp])
                for t in range(2):
                    c = cp * 2 + t
                    for s in range(NS):
                        pl = inproj_group(slx, t, s)
                        lru_chain(c, s, h * NS + s, pl, None, False)
                done_w()
            if sc_carry:
                for cp in range(4):
                    slc = get_w(win[12 + cp])
                    slxx = get_w(win[16 + cp])
                    for t in range(2):
                        c = cp * 2 + t
                        pc = inproj_group(slc, t, NS - 1)
                        px = inproj_group(slxx, t, NS - 1)
                        sc_chain(c, NS - 1, pc, px, None, True)
                    done_w()
                    done_w()

        def mixer_full(h):
            for s in range(NS):
                st = h * NS + s

                def P(c):
                    slot = get_w(win2[c])
                    pl = inproj_group(slot, 0, s)
                    pgt = inproj_group(slot, 1, s)
                    done_w()
                    return pl, pgt

                cur = P(0)
                lru_part_a(0, s, cur[0])
                for c in range(8):
                    nxt = P(c + 1) if c + 1 < 8 else None
                    lru_part_b(c, s, st, cur[1], True)
                    if nxt is not None:
                        lru_part_a(c + 1, s, nxt[0])
                    cur = nxt
                for cp in range(4):
                    slc = get_w(win[12 + cp])
                    slxx = get_w(win[16 + cp])
                    for t in range(2):
                        c = cp * 2 + t
                        pc = inproj_group(slc, t, s)
                        px = inproj_group(slxx, t, s)
                        sc_chain(c, s, pc, px, None, False, vi=2 + t)
                    done_w()
                    done_w()
                    slb = get_w(win[8 + cp])
                    for t in range(2):
                        c = cp * 2 + t
                        pb = inproj_group(slb, t, s)
                        sc_final(c, pb, 2 + t)
                    done_w()
                group_norm_apply(s)
            for j in range(8):
                slot = get_w(wout[j])
                for t in range(2):
                    m = 2 * j + t
                    for s in range(NS):
                        p = next_ps()
                        items = []
                        for k in range(NK):
                            o = (t * NK + k) * 128
                            items.append((mm(PS[p][:, :], W[:, slot, o:o + 128], XN[:, k, sl(s)], k == 0, k == NK - 1),
                                          (bW[slot], bXN[k][s])))
                        emit_group(items, (bPS[p],))
                        emit("dve", lambda e, p=p, m=m, s=s: e.tensor_tensor(
                            out=X[:, m, sl(s)], in0=PS[p][:, :], in1=X[:, m, sl(s)], op=ALU.add),
                            reads=(bPS[p], bX[m][s]), writes=(bX[m][s],))
                done_w()

        pending = []

        def run_pending(n):
            for _ in range(n):
                if pending:
                    pending.pop(0)()

        def program():
            del pending[:]
            for h in range(NH):
                full = h >= NPRE
                if not wstate["dry"]:
                    for s_ in range(NS):
                        for k in range(NK):
                            S.dma("sp", ("x", k, s_),
                                  lambda e, h=h, k=k, s_=s_: e.dma_start(
                                      out=X[:, k, sl(s_)], in_=xs[h, :, k * TH + s_ * TS:k * TH + (s_ + 1) * TS]),
                                  reads=(), writes=(bX[k][s_],))
                run_pending(6)
                norm_x_to_xn(V_G1)
                if h == 0:
                    dbg("d_xn1", XN[:, :, :], [128, NK, TH], BF16, [b for r in bXN for b in r])
                ffn(0, hook=lambda: run_pending(1))
                run_pending(len(pending))
                if h == 1:
                    dbg("d_hst", HST[:, :], [128, 8], F32, [bHST])
                    dbg("d_lc", LC[:, :, :], [128, 8, 4], F32, [bLC])
                if h == 0:
                    dbg("d_x1", X[:, :, :], [128, NK, TH], F32, [b for r in bX for b in r])
                if not full:
                    norm_x_to_xn(V_GMIX, alias=True)
                    if h == 0:
                        dbg("d_xn", XNB[:, :, :], [128, NK, TH], BF16, list(bY))
                    if h == NPRE - 1:
                        sc_carry_now()
                    pending.extend(prefix_units(h))
                else:
                    norm_x_to_xn(V_GMIX)
                    mixer_full(h)
                    norm_x_to_xn(V_G2)
                    ffn(1)
                    norm_x_final(V_GFIN)
                    if not wstate["dry"]:
                        for s_ in range(NS):
                            for k in range(NK):
                                S.dma("sp", ("x", k, s_),
                                      lambda e, h=h, k=k, s_=s_: e.dma_start(
                                          out=outd[h - NPRE, :, k * TH + s_ * TS:k * TH + (s_ + 1) * TS], in_=X[:, k, sl(s_)]),
                                      reads=(bX[k][s_],), writes=())

        wstate["dry"] = True
        program()
        wstate["dry"] = False
        wstate["next_use"] = 0
        ps_state["i"] = 0
        S.dma("sp", ("c", 0), lambda e: e.dma_start(out=VEC[:, :], in_=vecd[:, :]), writes=(bVEC,))
        S.dma("pool", ("c", 1), lambda e: e.dma_start(out=WG[:, :, :, :], in_=wgate.rearrange("p (t h j) -> p t h j", t=2, h=8),
                                                      max_dma_last_dim=8192), writes=(bWG,))
        S.op("dve", lambda e: e.memset(ONES[:, :], 1.0), writes=(bONES,))
        S.op("dve", lambda e: e.memset(LC[:, :, :], 0.0), writes=(bLC,))
        S.op("dve", lambda e: e.memset(SCC[:, :, :], 0.0), writes=(bSCC,))
        S.op("dve", lambda e: e.memset(HST[:, :], 0.0), writes=(bHST,))
        S.op("dve", lambda e: e.memset(CE[:, :], EPS), writes=(bDV,))
        S.op("dve", lambda e: e.memset(CQ[:, :], 0.25), writes=(bDV,))
        S.op("dve", lambda e: e.tensor_scalar(out=DV[:, DV_HBA:DV_HBA + 16], in0=VEC[:, V_BA:V_BA + 16], scalar1=0.5,
                                              scalar2=None, op0=ALU.mult), reads=(bVEC,), writes=(bDV,))
        bSPT = Buf()
        S.op("act", lambda e: e.activation(out=SPT[:, :], in_=VEC[:, V_LAM:V_LAM + 8], func=AF.Exp, scale=-1.0),
             reads=(bVEC,), writes=(bSPT,))
        S.op("act", lambda e: e.activation(out=SPT[:, :], in_=SPT[:, :], func=AF.Ln, bias=1.0),
             reads=(bSPT,), writes=(bSPT,), strict=True)
        S.op("dve", lambda e: e.tensor_scalar(out=DV[:, DV_N8:DV_N8 + 8], in0=SPT[:, :], scalar1=-8.0,
                                              scalar2=None, op0=ALU.mult), reads=(bSPT, bDV), writes=(bDV,), strict=True)
        S.op("dve", lambda e: e.tensor_scalar(out=DV[:, DV_N4:DV_N4 + 8], in0=SPT[:, :], scalar1=-4.0,
                                              scalar2=None, op0=ALU.mult), reads=(bSPT, bDV), writes=(bDV,), strict=True)
        wm_dma(0)
        for _ in range(NW):
            issue_wdma()
        program()
        assert wstate["next_use"] == len(wseq), (wstate["next_use"], len(wseq))
        fin = []
        for k in range(NK):
            for s_ in range(NS):
                fin.append((("x", k, s_), S.dmacnt[("x", k, s_)]))
        for key in S.dmacnt:
            if key[0] == "dbg":
                fin.append((key, 16))
        S.ops["sp"].append((fin, None, None))

        for key in list(S.dmacnt.keys()):
            sem(key)

        def replay(name, e):
            for waits, fn, inc in S.ops[name]:
                for k, v in waits:
                    e.wait_ge(sems[k], v)
                if fn is None:
                    continue
                ins = fn(e)
                if inc is not None:
                    ins.then_inc(sems[inc[0]], inc[1])

        with nc.Block() as block:
            @block.tensor
            def _(e):
                replay("pe", e)

            @block.scalar
            def _(e):
                replay("act", e)

            @block.vector
            def _(e):
                replay("dve", e)

            @block.gpsimd
            def _(e):
                replay("pool", e)

            @block.sync
            def _(e):
                replay("sp", e)
    _DBG["names"] = dbg_list
    return nc


def _prep_weights(inp):
    w = {}
    for i, pre in ((1, "ffn1"), (2, "ffn2")):
        wg = np.asarray(inp[pre + "_w_gate"][0], np.float32)
        wu = np.asarray(inp[pre + "_w_up"][0], np.float32)
        wd = np.asarray(inp[pre + "_w_down"][0], np.float32)
        g4 = wg.reshape(NK, 128, NF, 128).transpose(2, 1, 0, 3)
        u4 = wu.reshape(NK, 128, NF, 128).transpose(2, 1, 0, 3)
        w["wgu%d" % i] = np.ascontiguousarray(np.stack([g4, u4], axis=2)).reshape(NF, 128, 4096)
        d6 = wd.reshape(NG, FGS, 128, 2, 8, 128).transpose(0, 3, 2, 1, 4, 5)
        w["wd%d" % i] = np.ascontiguousarray(d6).reshape(2 * NG, 128, 4096)
    wi = np.asarray(inp["w_in"][0], np.float32)
    i4 = wi.reshape(NK, 128, 20, 2, 128).transpose(2, 1, 3, 0, 4)
    w["win"] = np.ascontiguousarray(i4).reshape(20, 128, 4096)
    i5 = wi.reshape(NK, 128, 40, 128).transpose(2, 1, 0, 3)
    w["win2"] = np.ascontiguousarray(np.stack([i5[0:8], i5[8:16]], axis=2)).reshape(8, 128, 4096)
    wo = np.asarray(inp["w_out"][0], np.float32)
    o4 = wo.reshape(NK, 128, 8, 2, 128).transpose(2, 1, 3, 0, 4)
    w["wout"] = np.ascontiguousarray(o4).reshape(8, 128, 4096)
    wa = np.asarray(inp["lru_w_a"][0], np.float32)
    wi_ = np.asarray(inp["lru_w_i"][0], np.float32)
    w["wgate"] = np.ascontiguousarray(np.stack([wa.transpose(1, 0, 2), wi_.transpose(1, 0, 2)], axis=1)).reshape(128, 2048)
    return w


def _vec_base(inp):
    v = np.zeros((128, NV), np.float32)

    def put(c0, arr, n):
        v[:, c0:c0 + n] = np.asarray(arr, np.float32).reshape(n, 128).T

    put(V_G1, inp["ffn1_norm"][0], 16)
    put(V_GMIX, inp["mix_norm"][0], 16)
    put(V_G2, inp["ffn2_norm"][0], 16)
    put(V_GFIN, inp["final_norm"], 16)
    put(V_GLRU, inp["lru_out_norm"][0], 8)
    put(V_GSC, inp["sc_out_norm"][0], 8)
    cw = np.asarray(inp["lru_conv_w"][0], np.float32)
    for tap in range(4):
        put(V_CW + tap * 8, cw[tap], 8)
    put(V_CB, inp["lru_conv_b"][0], 8)
    put(V_BA, np.asarray(inp["lru_b_a"][0]).reshape(-1), 8)
    put(V_BI, np.asarray(inp["lru_b_i"][0]).reshape(-1), 8)
    put(V_LAM, inp["lru_lambda"][0], 8)
    scw = np.asarray(inp["sc_conv_w"][0], np.float32)
    for tap in range(3):
        put(V_SCW + tap * 8, scw[tap], 8)
    return v


_NC_CACHE = {}


def kernel(**inputs):
    x = np.asarray(inputs["x"], np.float32)
    w = _prep_weights(inputs)
    vbase = _vec_base(inputs)
    in_maps = []
    for c in range(8):
        b, q = c // 4, c % 4
        npad = (3 - q) * 2048
        stream = np.zeros((NH * TH, D), np.float32)
        stream[npad:] = x[b, 0:(q + 1) * 2048]
        xs = stream.reshape(NH, TH, NK, 128).transpose(0, 3, 2, 1)
        v = vbase.copy()
        for st in range(16):
            v[:, V_MASK + st] = 0.0 if (st + 1) * TS <= npad else 1.0
        m = {"xs": np.ascontiguousarray(xs).reshape(NH, 128, NK * TH), "vec": v}
        m.update(w)
        in_maps.append(m)
    if "nc" not in _NC_CACHE:
        _NC_CACHE["nc"] = build_program()
    nc = _NC_CACHE["nc"]
    res = run_bass_kernel_spmd(nc, in_maps, core_ids=list(range(8)))
    out = np.empty((2, 8192, D), np.float32)
    for c in range(8):
        b, q = c // 4, c % 4
        o = np.asarray(res.results[c]["out"], np.float32).reshape(2, 128, NK, TH)
        out[b, q * 2048:(q + 1) * 2048] = o.transpose(0, 3, 2, 1).reshape(2 * TH, D)
    return out
```

```python
import numpy as np
from contextlib import ExitStack
import concourse.bass as bass
import concourse.mybir as mybir
from concourse.bass_utils import run_bass_kernel_spmd

F32 = mybir.dt.float32
BF16 = mybir.dt.bfloat16
AF = mybir.ActivationFunctionType
ALU = mybir.AluOpType

D = 2048
DFF = 5632
NK = 16
NF = 44
FGS = 4
NG = 11
TS = 512
NS = 2
TH = 1024
NH = 8
NPRE = 6
NW = 3
EPS = 1e-6

V_G1, V_GMIX, V_G2, V_GFIN = 0, 16, 32, 48
V_GLRU, V_GSC = 64, 72
V_CW = 80
V_CB = 112
V_BA = 120
V_BI = 128
V_LAM = 136
V_SCW = 144
V_MASK = 168
NV = 184
DV_HBA, DV_HBI, DV_N4, DV_N8 = 0, 8, 16, 24
NDV = 32


_DBG = {}


class Buf:
    __slots__ = ("w", "r")

    def __init__(self):
        self.w = None
        self.r = {}


class Sched:
    def __init__(self):
        self.ops = {e: [] for e in ("pe", "act", "dve", "pool", "sp")}
        self.cnt = {"pe": 0, "act": 0, "dve": 0}
        self.waited = {e: {} for e in self.ops}
        self.dmacnt = {}

    def _deps(self, eng, reads, writes, strict=False):
        d = {}

        def add(tok):
            if tok is None:
                return
            k, v = tok
            if k == ("p", eng):
                if not (strict and self.cnt[eng] - v < 3):
                    return
            if d.get(k, 0) < v:
                d[k] = v

        for b in reads:
            add(b.w)
        for b in writes:
            add(b.w)
            for k, v in b.r.items():
                add((k, v))
        waits = []
        for k, v in d.items():
            if self.waited[eng].get(k, 0) >= v:
                continue
            self.waited[eng][k] = v
            waits.append((k, v))
        return waits

    def _commit(self, tok, reads, writes):
        k, v = tok
        for b in reads:
            if b.r.get(k, 0) < v:
                b.r[k] = v
        for b in writes:
            b.w = tok
            b.r = {}

    def op(self, eng, fn, reads=(), writes=(), strict=False):
        waits = self._deps(eng, reads, writes, strict)
        self.cnt[eng] += 1
        tok = (("p", eng), self.cnt[eng])
        self.ops[eng].append((waits, fn, (("p", eng), 1)))
        self._commit(tok, reads, writes)
        return tok

    def dma(self, eng, semkey, fn, reads=(), writes=()):
        waits = self._deps(eng, reads, writes)
        self.dmacnt[semkey] = self.dmacnt.get(semkey, 0) + 16
        tok = (semkey, self.dmacnt[semkey])
        self.ops[eng].append((waits, fn, (semkey, 16)))
        self._commit(tok, reads, writes)
        return tok

    def mm_group(self, items, writes):
        n = len(items)
        allr = []
        for i, (fn, reads) in enumerate(items):
            waits = self._deps("pe", reads, writes if i == 0 else ())
            last = i == n - 1
            if last:
                self.cnt["pe"] += 1
            self.ops["pe"].append((waits, fn, (("p", "pe"), 1) if last else None))
            allr.extend(reads)
        tok = (("p", "pe"), self.cnt["pe"])
        self._commit(tok, allr, writes)
        return tok


def build_program(debug=False):
    nc = bass.Bass("TRN2", target_bir_lowering=False)
    S = Sched()

    xs = nc.dram_tensor("xs", [NH, 128, NK * TH], F32, kind="ExternalInput").ap()
    wgu = [nc.dram_tensor("wgu%d" % i, [NF, 128, 4096], F32, kind="ExternalInput").ap() for i in (1, 2)]
    wdn = [nc.dram_tensor("wd%d" % i, [2 * NG, 128, 4096], F32, kind="ExternalInput").ap() for i in (1, 2)]
    win = nc.dram_tensor("win", [20, 128, 4096], F32, kind="ExternalInput").ap()
    wout = nc.dram_tensor("wout", [8, 128, 4096], F32, kind="ExternalInput").ap()
    win2 = nc.dram_tensor("win2", [8, 128, 4096], F32, kind="ExternalInput").ap()
    wgate = nc.dram_tensor("wgate", [128, 2048], F32, kind="ExternalInput").ap()
    vecd = nc.dram_tensor("vec", [128, NV], F32, kind="ExternalInput").ap()
    outd = nc.dram_tensor("out", [2, 128, NK * TH], F32, kind="ExternalOutput").ap()

    dbg_list = []

    def dbg(name, ap, shape, dt, bufs):
        if not debug or wstate["dry"]:
            return
        d = nc.dram_tensor(name, shape, dt, kind="ExternalOutput").ap()
        dbg_list.append(name)
        S.dma("sp", ("dbg", len(dbg_list)), lambda e: e.dma_start(out=d, in_=ap), reads=tuple(bufs), writes=())

    es = ExitStack()
    with es:
        def sb(name, shape, dt):
            return es.enter_context(nc.sbuf_tensor(name, shape, dt))

        X = sb("X", [128, NK, TH], F32)
        XN = sb("XN", [128, NK, TH], BF16)
        H = sb("H", [128, 2, FGS, TH], BF16)
        W = sb("W", [128, NW, 4096], BF16)
        Y = sb("Y", [128, NK, TS], F32)
        XNB = Y.bitcast(BF16)
        WM = sb("WM", [128, 4096], BF16)
        NT = 9
        T = [sb("T%d" % i, [128, TS + 4], F32) for i in range(NT)]
        XCB = sb("XCB", [128, TS], BF16)
        SQ = [sb("SQ%d" % i, [128, TS], BF16) for i in range(2)]
        WG = sb("WG", [128, 2, 8, 128], BF16)
        VEC = sb("VEC", [128, NV], F32)
        DV = sb("DV", [128, NDV], F32)
        ONES = sb("ONES", [128, 128], BF16)
        LC = sb("LC", [128, 8, 4], F32)
        SCC = sb("SCC", [128, 8, 2], F32)
        HST = sb("HST", [128, 8], F32)
        CE = sb("CE", [128, 1], F32)
        CQ = sb("CQ", [128, 1], F32)
        SPT = sb("SPT", [128, 8], F32)
        PS = [es.enter_context(nc.psum_tensor("ps%d" % i, [128, TS], F32)) for i in range(8)]

        sems = {}

        def sem(key):
            if key not in sems:
                sems[key] = es.enter_context(nc.semaphore("s%d" % len(sems)))
            return sems[key]

        for k in (("p", "pe"), ("p", "act"), ("p", "dve")):
            sem(k)

        bX = [[Buf() for _ in range(NS)] for _ in range(NK)]
        bXN = [[Buf() for _ in range(NS)] for _ in range(NK)]
        bH = [[[Buf() for _ in range(NS)] for _ in range(FGS)] for _ in range(2)]
        bW = [Buf() for _ in range(NW)]
        bY = [Buf() for _ in range(NK)]
        bT = [Buf() for _ in range(NT)]
        bXCB = Buf()
        bSQ = [Buf(), Buf()]
        bWG = Buf()
        bWM = Buf()
        bVEC = Buf()
        bDV = Buf()
        bONES = Buf()
        bLC = Buf()
        bSCC = Buf()
        bHST = Buf()
        bPS = [Buf() for _ in range(8)]

        def col(c0, n=1):
            return VEC[:, c0:c0 + n]

        def dcol(c0, n=1):
            return DV[:, c0:c0 + n]

        ps_state = {"i": 0}

        def next_ps():
            i = ps_state["i"]
            ps_state["i"] = (i + 1) % 8
            return i

        wseq = []
        wstate = {"next_use": 0, "next_dma": 0, "dry": True}

        def issue_wdma():
            i = wstate["next_dma"]
            if i >= len(wseq):
                return
            wstate["next_dma"] = i + 1
            slot = i % NW
            src = wseq[i]
            S.dma("pool", ("w", slot),
                  lambda e, slot=slot, src=src: e.dma_start(out=W[:, slot, :], in_=src, max_dma_last_dim=8192),
                  reads=(), writes=(bW[slot],))

        def get_w(src):
            i = wstate["next_use"]
            wstate["next_use"] = i + 1
            if wstate["dry"]:
                wseq.append(src)
            else:
                assert wstate["next_dma"] > i, (i, wstate["next_dma"])
            return i % NW

        def done_w():
            if not wstate["dry"]:
                issue_wdma()

        def emit(eng, fn, reads=(), writes=(), strict=False):
            if wstate["dry"]:
                return
            S.op(eng, fn, reads, writes, strict)

        def emit_group(items, writes):
            if wstate["dry"]:
                return
            S.mm_group(items, writes)

        def mm(out_ap, lhsT, rhs, start, stop):
            return lambda e: e.matmul(out_ap, lhsT, rhs, start=start, stop=stop)

        def sl(s):
            return slice(s * TS, (s + 1) * TS)

        def norm_generic(src_ap_fn, src_bufs, nchunks, inv_n, apply_fn):
            p = next_ps()
            if not wstate["dry"]:
                for k in range(nchunks):
                    q = k % 2
                    S.op("act", lambda e, k=k, q=q: e.activation(out=SQ[q][:, :], in_=src_ap_fn(k), func=AF.Square),
                         reads=(src_bufs[k],), writes=(bSQ[q],))
                    waits = S._deps("pe", (bSQ[q], bONES), (bPS[p],) if k == 0 else ())
                    last = k == nchunks - 1
                    S.cnt["pe"] += 1
                    tok = (("p", "pe"), S.cnt["pe"])
                    S.ops["pe"].append((waits, mm(PS[p][:, :], ONES[:, :], SQ[q][:, :], k == 0, last), (("p", "pe"), 1)))
                    S._commit(tok, (bSQ[q], bONES), (bPS[p],) if last else ())
                    if not last:
                        pass
                S.op("act", lambda e: e.activation(out=T[8][:, 0:TS], in_=PS[p][:, :], func=AF.Sqrt, scale=inv_n, bias=EPSC[:, 0:1]),
                     reads=(bPS[p], bDV), writes=(bT[8],))
                S.op("dve", lambda e: e.reciprocal(out=T[8][:, 0:TS], in_=T[8][:, 0:TS]), reads=(bT[8],), writes=(bT[8],))
            apply_fn(T[8][:, 0:TS], bT[8])

        EPSC = CE[:, 0:1]

        def norm_x_to_xn(gbase, alias=False):
            for s in range(NS):
                def apply(rs, brs, s=s):
                    for k in range(NK):
                        dst = XNB if alias else XN
                        emit("dve", lambda e, k=k, s=s, dst=dst: e.scalar_tensor_tensor(
                            out=dst[:, k, sl(s)], in0=X[:, k, sl(s)], scalar=col(gbase + k), in1=rs,
                            op0=ALU.mult, op1=ALU.mult),
                            reads=(bX[k][s], brs, bVEC), writes=((bY[k],) if alias else (bXN[k][s],)))
                norm_generic(lambda k, s=s: X[:, k, sl(s)], [bX[k][s] for k in range(NK)], NK, 1.0 / D, apply)

        def norm_x_final(gbase):
            for s in range(NS):
                def apply(rs, brs, s=s):
                    for k in range(NK):
                        emit("dve", lambda e, k=k, s=s: e.scalar_tensor_tensor(
                            out=X[:, k, sl(s)], in0=X[:, k, sl(s)], scalar=col(gbase + k), in1=rs,
                            op0=ALU.mult, op1=ALU.mult),
                            reads=(bX[k][s], brs, bVEC), writes=(bX[k][s],))
                norm_generic(lambda k, s=s: X[:, k, sl(s)], [bX[k][s] for k in range(NK)], NK, 1.0 / D, apply)

        def ffn(fi, hook=None):
            def stage1(g):
                hb = g % 2
                for fp in range(FGS // 2):
                    fls = (2 * fp, 2 * fp + 1)
                    slots = [get_w(wgu[fi][g * FGS + fl]) for fl in fls]
                    for s in range(NS):
                        for fl, slot in zip(fls, slots):
                            f = g * FGS + fl
                            pg = next_ps()
                            pu = next_ps()
                            for (t, p) in ((0, pg), (1, pu)):
                                items = []
                                for k in range(NK):
                                    o = (t * NK + k) * 128
                                    items.append((mm(PS[p][:, :], W[:, slot, o:o + 128], XN[:, k, sl(s)], k == 0, k == NK - 1),
                                                  (bW[slot], bXN[k][s])))
                                emit_group(items, (bPS[p],))
                            tq = (f * NS + s) % 2
                            emit("act", lambda e, pg=pg, tq=tq: e.activation(out=T[6 + tq][:, 0:TS], in_=PS[pg][:, :], func=AF.Silu),
                                 reads=(bPS[pg],), writes=(bT[6 + tq],))
                            emit("dve", lambda e, pu=pu, tq=tq, hb=hb, fl=fl, s=s: e.tensor_tensor(
                                out=H[:, hb, fl, sl(s)], in0=PS[pu][:, :], in1=T[6 + tq][:, 0:TS], op=ALU.mult),
                                reads=(bPS[pu], bT[6 + tq]), writes=(bH[hb][fl][s],))
                    done_w()
                    done_w()
                    if hook is not None:
                        hook()

            def stage2(g):
                hb = g % 2
                for mh in range(2):
                    slot = get_w(wdn[fi][g * 2 + mh])
                    for m8 in range(8):
                        m = mh * 8 + m8
                        for s in range(NS):
                            p = next_ps()
                            items = []
                            for fl in range(FGS):
                                o = (fl * 8 + m8) * 128
                                items.append((mm(PS[p][:, :], W[:, slot, o:o + 128], H[:, hb, fl, sl(s)], fl == 0, fl == FGS - 1),
                                              (bW[slot], bH[hb][fl][s])))
                            emit_group(items, (bPS[p],))
                            emit("dve", lambda e, p=p, m=m, s=s: e.scalar_tensor_tensor(
                                out=X[:, m, sl(s)], in0=PS[p][:, :], scalar=0.5, in1=X[:, m, sl(s)],
                                op0=ALU.mult, op1=ALU.add),
                                reads=(bPS[p], bX[m][s]), writes=(bX[m][s],))
                    done_w()

            for g in range(NG):
                stage1(g)
                if g >= 1:
                    stage2(g - 1)
            stage2(NG - 1)

        def lru_part_a(c, s, pl):
            CB, XC, TR, TI, A2, HS = T[0], T[1], T[2], T[3], T[4], T[5]
            emit("dve", lambda e: e.tensor_copy(out=CB[:, 0:3], in_=LC[:, c, 0:3]), reads=(bLC,), writes=(bT[0],))
            emit("act", lambda e: e.activation(out=CB[:, 3:3 + TS], in_=PS[pl][:, :], func=AF.Copy),
                 reads=(bPS[pl],), writes=(bT[0],))
            emit("dve", lambda e: e.tensor_scalar(out=XC[:, 0:TS], in0=CB[:, 3:3 + TS], scalar1=col(V_CW + 3 * 8 + c),
                                                  scalar2=col(V_CB + c), op0=ALU.mult, op1=ALU.add),
                 reads=(bT[0], bVEC), writes=(bT[1],))
            for tap in (2, 1, 0):
                emit("dve", lambda e, tap=tap: e.scalar_tensor_tensor(
                    out=XC[:, 0:TS], in0=CB[:, tap:tap + TS], scalar=col(V_CW + tap * 8 + c), in1=XC[:, 0:TS],
                    op0=ALU.mult, op1=ALU.add), reads=(bT[0], bT[1], bVEC), writes=(bT[1],))
            emit("dve", lambda e: e.tensor_copy(out=LC[:, c, 0:3], in_=CB[:, TS:TS + 3]), reads=(bT[0],), writes=(bLC,))
            emit("act", lambda e: e.activation(out=XCB[:, :], in_=XC[:, 0:TS], func=AF.Copy), reads=(bT[1],), writes=(bXCB,))

        def lru_part_b(c, s, st, pgt, full):
            CB, XC, TR, TI, A2, HS = T[0], T[1], T[2], T[3], T[4], T[5]
            pr = next_ps()
            pi = next_ps()
            emit_group([(mm(PS[pr][:, :], WG[:, 0, c, :], XCB[:, :], True, True), (bWG, bXCB))], (bPS[pr],))
            emit_group([(mm(PS[pi][:, :], WG[:, 1, c, :], XCB[:, :], True, True), (bWG, bXCB))], (bPS[pi],))
            emit("act", lambda e: e.activation(out=TR[:, 0:TS], in_=PS[pr][:, :], func=AF.Tanh, scale=0.5, bias=dcol(DV_HBA + c)),
                 reads=(bPS[pr], bDV), writes=(bT[2],))
            emit("act", lambda e: e.activation(out=TI[:, 0:TS], in_=PS[pi][:, :], func=AF.Tanh, scale=0.5, bias=dcol(DV_HBI + c)),
                 reads=(bPS[pi], bDV), writes=(bT[3],))
            emit("act", lambda e: e.activation(out=A2[:, 0:TS], in_=TR[:, 0:TS], func=AF.Exp, scale=dcol(DV_N8 + c), bias=dcol(DV_N8 + c)),
                 reads=(bT[2], bDV), writes=(bT[4],))
            emit("act", lambda e: e.activation(out=TR[:, 0:TS], in_=TR[:, 0:TS], func=AF.Exp, scale=dcol(DV_N4 + c), bias=dcol(DV_N4 + c)),
                 reads=(bT[2], bDV), writes=(bT[2],))
            emit("dve", lambda e: e.tensor_scalar(out=A2[:, 0:TS], in0=A2[:, 0:TS], scalar1=0.9999999, scalar2=-0.25,
                                                  op0=ALU.min, op1=ALU.mult), reads=(bT[4],), writes=(bT[4],))
            emit("act", lambda e: e.activation(out=A2[:, 0:TS], in_=A2[:, 0:TS], func=AF.Sqrt, scale=1.0, bias=CQ[:, 0:1]),
                 reads=(bT[4], bDV), writes=(bT[4],))
            emit("dve", lambda e: e.scalar_tensor_tensor(out=TI[:, 0:TS], in0=TI[:, 0:TS], scalar=1.0, in1=XC[:, 0:TS],
                                                         op0=ALU.add, op1=ALU.mult), reads=(bT[3], bT[1]), writes=(bT[3],))
            emit("dve", lambda e: e.tensor_tensor(out=TI[:, 0:TS], in0=TI[:, 0:TS], in1=A2[:, 0:TS], op=ALU.mult),
                 reads=(bT[3], bT[4]), writes=(bT[3],))
            emit("dve", lambda e: e.tensor_tensor_scan(out=HS[:, 0:TS], data0=TR[:, 0:TS], data1=TI[:, 0:TS],
                                                       initial=HST[:, c:c + 1], op0=ALU.mult, op1=ALU.add),
                 reads=(bT[2], bT[3], bHST), writes=(bT[5],))
            emit("dve", lambda e: e.tensor_scalar(out=HST[:, c:c + 1], in0=HS[:, TS - 1:TS], scalar1=col(V_MASK + st),
                                                  scalar2=None, op0=ALU.mult), reads=(bT[5], bVEC), writes=(bHST,), strict=True)
            if full:
                G = A2
                emit("act", lambda e: e.activation(out=G[:, 0:TS], in_=PS[pgt][:, :], func=AF.Square),
                     reads=(bPS[pgt],), writes=(bT[4],))
                emit("dve", lambda e: e.tensor_scalar(out=G[:, 0:TS], in0=G[:, 0:TS], scalar1=0.044715, scalar2=1.0,
                                                      op0=ALU.mult, op1=ALU.add), reads=(bT[4],), writes=(bT[4],))
                emit("dve", lambda e: e.tensor_tensor(out=G[:, 0:TS], in0=PS[pgt][:, :], in1=G[:, 0:TS], op=ALU.mult),
                     reads=(bPS[pgt], bT[4]), writes=(bT[4],))
                emit("act", lambda e: e.activation(out=G[:, 0:TS], in_=G[:, 0:TS], func=AF.Tanh, scale=0.7978845608028654),
                     reads=(bT[4],), writes=(bT[4],))
                emit("dve", lambda e: e.scalar_tensor_tensor(out=G[:, 0:TS], in0=G[:, 0:TS], scalar=1.0, in1=PS[pgt][:, :],
                                                             op0=ALU.add, op1=ALU.mult), reads=(bT[4], bPS[pgt]), writes=(bT[4],))
                emit("dve", lambda e: e.scalar_tensor_tensor(out=Y[:, c, :], in0=G[:, 0:TS], scalar=0.5, in1=HS[:, 0:TS],
                                                             op0=ALU.mult, op1=ALU.mult), reads=(bT[4], bT[5]), writes=(bY[c],))

        def lru_chain(c, s, st, pl, pgt, full):
            lru_part_a(c, s, pl)
            lru_part_b(c, s, st, pgt, full)

        def inproj_group(slot, t, s):
            p = next_ps()
            items = []
            for k in range(NK):
                o = (t * NK + k) * 128
                items.append((mm(PS[p][:, :], W[:, slot, o:o + 128], XN[:, k, sl(s)], k == 0, k == NK - 1),
                              (bW[slot], bXN[k][s])))
            emit_group(items, (bPS[p],))
            return p

        def inproj_group2(wap_fn, wbuf, xsrc, xbufs, t, s):
            p = next_ps()
            items = []
            for k in range(NK):
                o = (t * NK + k) * 128
                items.append((mm(PS[p][:, :], wap_fn(o), xsrc[:, k, sl(s)], k == 0, k == NK - 1), (wbuf, xbufs[k])))
            emit_group(items, (bPS[p],))
            return p

        def wm_dma(cp):
            if wstate["dry"]:
                return
            S.dma("pool", ("wm", 0),
                  lambda e, cp=cp: e.dma_start(out=WM[:, :], in_=win[cp], max_dma_last_dim=8192),
                  reads=(), writes=(bWM,))

        def prefix_units(h):
            keys = [(cp, t, s) for cp in range(4) for t in range(2) for s in range(NS)]

            def part_a(i):
                cp, t, s = keys[i]
                pl = inproj_group2(lambda o: WM[:, o:o + 128], bWM, XNB, bY, t, s)
                lru_part_a(cp * 2 + t, s, pl)
                if t == 1 and s == NS - 1:
                    if cp < 3:
                        wm_dma(cp + 1)
                    elif h + 1 < NPRE:
                        wm_dma(0)

            def part_b(i):
                cp, t, s = keys[i]
                lru_part_b(cp * 2 + t, s, h * NS + s, None, False)

            units = []
            for i in range(len(keys) + 1):
                def unit(i=i):
                    if i > 0:
                        part_b(i - 1)
                    if i < len(keys):
                        part_a(i)
                units.append(unit)
            return units

        def sc_carry_now():
            for cp in range(4):
                slc = get_w(win[12 + cp])
                slxx = get_w(win[16 + cp])
                for t in range(2):
                    c = cp * 2 + t
                    pc = inproj_group2(lambda o, slc=slc: W[:, slc, o:o + 128], bW[slc], XNB, bY, t, NS - 1)
                    px = inproj_group2(lambda o, slxx=slxx: W[:, slxx, o:o + 128], bW[slxx], XNB, bY, t, NS - 1)
                    sc_chain(c, NS - 1, pc, px, None, True)
                done_w()
                done_w()

        def sc_chain(c, s, pc, px, pb, carry_only, vi=2):
            SX, CB2, Vv = T[0], T[1], T[vi]
            emit("act", lambda e: e.activation(out=SX[:, 0:TS], in_=PS[px][:, :], func=AF.Copy), reads=(bPS[px],), writes=(bT[0],))
            emit("dve", lambda e: e.tensor_copy(out=CB2[:, 0:2], in_=SCC[:, c, 0:2]), reads=(bSCC,), writes=(bT[1],))
            emit("dve", lambda e: e.tensor_tensor(out=CB2[:, 2:2 + TS], in0=PS[pc][:, :], in1=SX[:, 0:TS], op=ALU.mult),
                 reads=(bPS[pc], bT[0]), writes=(bT[1],))
            emit("dve", lambda e: e.tensor_copy(out=SCC[:, c, 0:2], in_=CB2[:, TS:TS + 2]), reads=(bT[1],), writes=(bSCC,), strict=True)
            if carry_only:
                return
            emit("dve", lambda e: e.tensor_scalar(out=Vv[:, 0:TS], in0=CB2[:, 2:2 + TS], scalar1=col(V_SCW + 2 * 8 + c),
                                                  scalar2=None, op0=ALU.mult), reads=(bT[1], bVEC), writes=(bT[vi],))
            for tap in (1, 0):
                emit("dve", lambda e, tap=tap: e.scalar_tensor_tensor(
                    out=Vv[:, 0:TS], in0=CB2[:, tap:tap + TS], scalar=col(V_SCW + tap * 8 + c), in1=Vv[:, 0:TS],
                    op0=ALU.mult, op1=ALU.add), reads=(bT[1], bT[vi], bVEC), writes=(bT[vi],))
            if pb is not None:
                sc_final(c, pb, vi)

        def sc_final(c, pb, vi):
            emit("dve", lambda e: e.tensor_tensor(out=Y[:, 8 + c, :], in0=PS[pb][:, :], in1=T[vi][:, 0:TS], op=ALU.mult),
                 reads=(bPS[pb], bT[vi]), writes=(bY[8 + c],))

        def group_norm_apply(s):
            for grp in range(2):
                gbase = V_GLRU if grp == 0 else V_GSC

                def apply(rs, brs, grp=grp, gbase=gbase):
                    for c in range(8):
                        k = grp * 8 + c
                        emit("dve", lambda e, k=k, c=c: e.scalar_tensor_tensor(
                            out=XN[:, k, sl(s)], in0=Y[:, k, :], scalar=col(gbase + c), in1=rs,
                            op0=ALU.mult, op1=ALU.mult), reads=(bY[k], brs, bVEC), writes=(bXN[k][s],))
                norm_generic(lambda c, grp=grp: Y[:, grp * 8 + c, :], [bY[grp * 8 + c] for c in range(8)], 8, 1.0 / 1024, apply)

        def mixer_prefix(h, sc_carry):
            for cp in range(4):
                slx = get_w(win[cp])
                for t in range(2):
                    c = cp * 2 + t
                    for s in range(NS):
                        pl = inproj_group(slx, t, s)
                        lru_chain(c, s, h * NS + s, pl, None, False)
                done_w()
            if sc_carry:
                for cp in range(4):
                    slc = get_w(win[12 + cp])
                    slxx = get_w(win[16 + cp])
                    for t in range(2):
                        c = cp * 2 + t
                        pc = inproj_group(slc, t, NS - 1)
                        px = inproj_group(slxx, t, NS - 1)
                        sc_chain(c, NS - 1, pc, px, None, True)
                    done_w()
                    done_w()

        def mixer_full(h):
            for s in range(NS):
                st = h * NS + s

                def P(c):
                    slot = get_w(win2[c])
                    pl = inproj_group(slot, 0, s)
                    pgt = inproj_group(slot, 1, s)
                    done_w()
                    return pl, pgt

                cur = P(0)
                lru_part_a(0, s, cur[0])
                for c in range(8):
                    nxt = P(c + 1) if c + 1 < 8 else None
                    lru_part_b(c, s, st, cur[1], True)
                    if nxt is not None:
                        lru_part_a(c + 1, s, nxt[0])
                    cur = nxt
                for cp in range(4):
                    slc = get_w(win[12 + cp])
                    slxx = get_w(win[16 + cp])
                    for t in range(2):
                        c = cp * 2 + t
                        pc = inproj_group(slc, t, s)
                        px = inproj_group(slxx, t, s)
                        sc_chain(c, s, pc, px, None, False, vi=2 + t)
                    done_w()
                    done_w()
                    slb = get_w(win[8 + cp])
                    for t in range(2):
                        c = cp * 2 + t
                        pb = inproj_group(slb, t, s)
                        sc_final(c, pb, 2 + t)
                    done_w()
                group_norm_apply(s)
            for j in range(8):
                slot = get_w(wout[j])
                for t in range(2):
                    m = 2 * j + t
                    for s in range(NS):
                        p = next_ps()
                        items = []
                        for k in range(NK):
                            o = (t * NK + k) * 128
                            items.append((mm(PS[p][:, :], W[:, slot, o:o + 128], XN[:, k, sl(s)], k == 0, k == NK - 1),
                                          (bW[slot], bXN[k][s])))
                        emit_group(items, (bPS[p],))
                        emit("dve", lambda e, p=p, m=m, s=s: e.tensor_tensor(
                            out=X[:, m, sl(s)], in0=PS[p][:, :], in1=X[:, m, sl(s)], op=ALU.add),
                            reads=(bPS[p], bX[m][s]), writes=(bX[m][s],))
                done_w()

        pending = []

        def run_pending(n):
            for _ in range(n):
                if pending:
                    pending.pop(0)()

        def program():
            del pending[:]
            for h in range(NH):
                full = h >= NPRE
                if not wstate["dry"]:
                    for s_ in range(NS):
                        for k in range(NK):
                            S.dma("sp", ("x", k, s_),
                                  lambda e, h=h, k=k, s_=s_: e.dma_start(
                                      out=X[:, k, sl(s_)], in_=xs[h, :, k * TH + s_ * TS:k * TH + (s_ + 1) * TS]),
                                  reads=(), writes=(bX[k][s_],))
                run_pending(2)
                norm_x_to_xn(V_G1)
                if h == 0:
                    dbg("d_xn1", XN[:, :, :], [128, NK, TH], BF16, [b for r in bXN for b in r])
                ffn(0, hook=lambda: run_pending(1))
                run_pending(len(pending))
                if h == 1:
                    dbg("d_hst", HST[:, :], [128, 8], F32, [bHST])
                    dbg("d_lc", LC[:, :, :], [128, 8, 4], F32, [bLC])
                if h == 0:
                    dbg("d_x1", X[:, :, :], [128, NK, TH], F32, [b for r in bX for b in r])
                if not full:
                    norm_x_to_xn(V_GMIX, alias=True)
                    if h == 0:
                        dbg("d_xn", XNB[:, :, :], [128, NK, TH], BF16, list(bY))
                    if h == NPRE - 1:
                        sc_carry_now()
                    pending.extend(prefix_units(h))
                else:
                    norm_x_to_xn(V_GMIX)
                    mixer_full(h)
                    norm_x_to_xn(V_G2)
                    ffn(1)
                    norm_x_final(V_GFIN)
                    if not wstate["dry"]:
                        for s_ in range(NS):
                            for k in range(NK):
                                S.dma("sp", ("x", k, s_),
                                      lambda e, h=h, k=k, s_=s_: e.dma_start(
                                          out=outd[h - NPRE, :, k * TH + s_ * TS:k * TH + (s_ + 1) * TS], in_=X[:, k, sl(s_)]),
                                      reads=(bX[k][s_],), writes=())

        wstate["dry"] = True
        program()
        wstate["dry"] = False
        wstate["next_use"] = 0
        ps_state["i"] = 0
        S.dma("sp", ("c", 0), lambda e: e.dma_start(out=VEC[:, :], in_=vecd[:, :]), writes=(bVEC,))
        S.dma("pool", ("c", 1), lambda e: e.dma_start(out=WG[:, :, :, :], in_=wgate.rearrange("p (t h j) -> p t h j", t=2, h=8),
                                                      max_dma_last_dim=8192), writes=(bWG,))
        S.op("dve", lambda e: e.memset(ONES[:, :], 1.0), writes=(bONES,))
        S.op("dve", lambda e: e.memset(LC[:, :, :], 0.0), writes=(bLC,))
        S.op("dve", lambda e: e.memset(SCC[:, :, :], 0.0), writes=(bSCC,))
        S.op("dve", lambda e: e.memset(HST[:, :], 0.0), writes=(bHST,))
        S.op("dve", lambda e: e.memset(CE[:, :], EPS), writes=(bDV,))
        S.op("dve", lambda e: e.memset(CQ[:, :], 0.25), writes=(bDV,))
        S.op("dve", lambda e: e.tensor_scalar(out=DV[:, DV_HBA:DV_HBA + 16], in0=VEC[:, V_BA:V_BA + 16], scalar1=0.5,
                                              scalar2=None, op0=ALU.mult), reads=(bVEC,), writes=(bDV,))
        bSPT = Buf()
        S.op("act", lambda e: e.activation(out=SPT[:, :], in_=VEC[:, V_LAM:V_LAM + 8], func=AF.Exp, scale=-1.0),
             reads=(bVEC,), writes=(bSPT,))
        S.op("act", lambda e: e.activation(out=SPT[:, :], in_=SPT[:, :], func=AF.Ln, bias=1.0),
             reads=(bSPT,), writes=(bSPT,), strict=True)
        S.op("dve", lambda e: e.tensor_scalar(out=DV[:, DV_N8:DV_N8 + 8], in0=SPT[:, :], scalar1=-8.0,
                                              scalar2=None, op0=ALU.mult), reads=(bSPT, bDV), writes=(bDV,), strict=True)
        S.op("dve", lambda e: e.tensor_scalar(out=DV[:, DV_N4:DV_N4 + 8], in0=SPT[:, :], scalar1=-4.0,
                                              scalar2=None, op0=ALU.mult), reads=(bSPT, bDV), writes=(bDV,), strict=True)
        wm_dma(0)
        for _ in range(NW):
            issue_wdma()
        program()
        assert wstate["next_use"] == len(wseq), (wstate["next_use"], len(wseq))
        fin = []
        for k in range(NK):
            for s_ in range(NS):
                fin.append((("x", k, s_), S.dmacnt[("x", k, s_)]))
        for key in S.dmacnt:
            if key[0] == "dbg":
                fin.append((key, 16))
        S.ops["sp"].append((fin, None, None))

        for key in list(S.dmacnt.keys()):
            sem(key)

        def replay(name, e):
            for waits, fn, inc in S.ops[name]:
                for k, v in waits:
                    e.wait_ge(sems[k], v)
                if fn is None:
                    continue
                ins = fn(e)
                if inc is not None:
                    ins.then_inc(sems[inc[0]], inc[1])

        with nc.Block() as block:
            @block.tensor
            def _(e):
                replay("pe", e)

            @block.scalar
            def _(e):
                replay("act", e)

            @block.vector
            def _(e):
                replay("dve", e)

            @block.gpsimd
            def _(e):
                replay("pool", e)

            @block.sync
            def _(e):
                replay("sp", e)
    _DBG["names"] = dbg_list
    return nc


def _prep_weights(inp):
    w = {}
    for i, pre in ((1, "ffn1"), (2, "ffn2")):
        wg = np.asarray(inp[pre + "_w_gate"][0], np.float32)
        wu = np.asarray(inp[pre + "_w_up"][0], np.float32)
        wd = np.asarray(inp[pre + "_w_down"][0], np.float32)
        g4 = wg.reshape(NK, 128, NF, 128).transpose(2, 1, 0, 3)
        u4 = wu.reshape(NK, 128, NF, 128).transpose(2, 1, 0, 3)
        w["wgu%d" % i] = np.ascontiguousarray(np.stack([g4, u4], axis=2)).reshape(NF, 128, 4096)
        d6 = wd.reshape(NG, FGS, 128, 2, 8, 128).transpose(0, 3, 2, 1, 4, 5)
        w["wd%d" % i] = np.ascontiguousarray(d6).reshape(2 * NG, 128, 4096)
    wi = np.asarray(inp["w_in"][0], np.float32)
    i4 = wi.reshape(NK, 128, 20, 2, 128).transpose(2, 1, 3, 0, 4)
    w["win"] = np.ascontiguousarray(i4).reshape(20, 128, 4096)
    i5 = wi.reshape(NK, 128, 40, 128).transpose(2, 1, 0, 3)
    w["win2"] = np.ascontiguousarray(np.stack([i5[0:8], i5[8:16]], axis=2)).reshape(8, 128, 4096)
    wo = np.asarray(inp["w_out"][0], np.float32)
    o4 = wo.reshape(NK, 128, 8, 2, 128).transpose(2, 1, 3, 0, 4)
    w["wout"] = np.ascontiguousarray(o4).reshape(8, 128, 4096)
    wa = np.asarray(inp["lru_w_a"][0], np.float32)
    wi_ = np.asarray(inp["lru_w_i"][0], np.float32)
    w["wgate"] = np.ascontiguousarray(np.stack([wa.transpose(1, 0, 2), wi_.transpose(1, 0, 2)], axis=1)).reshape(128, 2048)
    return w


def _vec_base(inp):
    v = np.zeros((128, NV), np.float32)

    def put(c0, arr, n):
        v[:, c0:c0 + n] = np.asarray(arr, np.float32).reshape(n, 128).T

    put(V_G1, inp["ffn1_norm"][0], 16)
    put(V_GMIX, inp["mix_norm"][0], 16)
    put(V_G2, inp["ffn2_norm"][0], 16)
    put(V_GFIN, inp["final_norm"], 16)
    put(V_GLRU, inp["lru_out_norm"][0], 8)
    put(V_GSC, inp["sc_out_norm"][0], 8)
    cw = np.asarray(inp["lru_conv_w"][0], np.float32)
    for tap in range(4):
        put(V_CW + tap * 8, cw[tap], 8)
    put(V_CB, inp["lru_conv_b"][0], 8)
    put(V_BA, np.asarray(inp["lru_b_a"][0]).reshape(-1), 8)
    put(V_BI, np.asarray(inp["lru_b_i"][0]).reshape(-1), 8)
    put(V_LAM, inp["lru_lambda"][0], 8)
    scw = np.asarray(inp["sc_conv_w"][0], np.float32)
    for tap in range(3):
        put(V_SCW + tap * 8, scw[tap], 8)
    return v


_NC_CACHE = {}


def kernel(**inputs):
    x = np.asarray(inputs["x"], np.float32)
    w = _prep_weights(inputs)
    vbase = _vec_base(inputs)
    in_maps = []
    for c in range(8):
        b, q = c // 4, c % 4
        npad = (3 - q) * 2048
        stream = np.zeros((NH * TH, D), np.float32)
        stream[npad:] = x[b, 0:(q + 1) * 2048]
        xs = stream.reshape(NH, TH, NK, 128).transpose(0, 3, 2, 1)
        v = vbase.copy()
        for st in range(16):
            v[:, V_MASK + st] = 0.0 if (st + 1) * TS <= npad else 1.0
        m = {"xs": np.ascontiguousarray(xs).reshape(NH, 128, NK * TH), "vec": v}
        m.update(w)
        in_maps.append(m)
    if "nc" not in _NC_CACHE:
        _NC_CACHE["nc"] = build_program()
    nc = _NC_CACHE["nc"]
    res = run_bass_kernel_spmd(nc, in_maps, core_ids=list(range(8)))
    out = np.empty((2, 8192, D), np.float32)
    for c in range(8):
        b, q = c // 4, c % 4
        o = np.asarray(res.results[c]["out"], np.float32).reshape(2, 128, NK, TH)
        out[b, q * 2048:(q + 1) * 2048] = o.transpose(0, 3, 2, 1).reshape(2 * TH, D)
    return out
```
